# Optimizing a Trainium2 kernel written in Bass

```python
import math
import jax
import jax.numpy as jnp
from jax import lax
import numpy as np

D_MODEL = 1024
BATCH = 2
SEQ = 16384
DEPTH = 4

N_NSA_HEADS = 8
N_KV_HEADS = 2
GQA_GROUP = N_NSA_HEADS // N_KV_HEADS
HEAD_DIM = 64
NSA_WIDTH = N_NSA_HEADS * HEAD_DIM
KV_WIDTH = N_KV_HEADS * HEAD_DIM
N_BRANCHES = 3
N_GATES = N_BRANCHES * N_NSA_HEADS
CMP_BLOCK = 32
CMP_STRIDE = 16
CMP_HIDDEN = 4 * HEAD_DIM
SLC_BLOCK = 64
SLC_TOPK = 16
N_LOCAL_BLOCKS = 2
WINDOW = 512
Q_BLOCK = 128
GMLP_WIDTH = D_MODEL - NSA_WIDTH
N_GMLP_GROUPS = 8
GMLP_GROUP_DIM = GMLP_WIDTH // N_GMLP_GROUPS
GMLP_CHUNK = 128
IN_WIDTH = NSA_WIDTH + 6 * KV_WIDTH + N_GATES + 2 * GMLP_WIDTH
D_FF = ((8 * D_MODEL + 3 * 256 - 1) // (3 * 256)) * 256
N_BUCKETS = 32
REL_MAX_DISTANCE = 128
RMS_EPS = 1e-6
LN_EPS = 1e-5

kernel_name = "nsa_gmlp_parallel_hybrid"


def rms_norm(x, g):
    xf = x.astype(jnp.float32)
    y = xf * lax.rsqrt(jnp.mean(xf * xf, axis=-1, keepdims=True) + RMS_EPS)
    return (y * g.astype(jnp.float32)).astype(x.dtype)


def layer_norm(x, g, b):
    xf = x.astype(jnp.float32)
    mu = jnp.mean(xf, axis=-1, keepdims=True)
    var = jnp.mean(jnp.square(xf - mu), axis=-1, keepdims=True)
    y = (xf - mu) * lax.rsqrt(var + LN_EPS)
    return (y * g.astype(jnp.float32) + b.astype(jnp.float32)).astype(x.dtype)


def t5_bucket(dist):
    n = jnp.maximum(dist, 0)
    max_exact = N_BUCKETS // 2
    nf = jnp.maximum(n, max_exact).astype(jnp.float32)
    large = max_exact + (jnp.log(nf / max_exact) / math.log(REL_MAX_DISTANCE / max_exact)
                         * (N_BUCKETS - max_exact)).astype(jnp.int32)
    return jnp.where(n < max_exact, n, jnp.minimum(large, N_BUCKETS - 1))


def bias_shared(table, dist):
    b = table.astype(jnp.float32)[t5_bucket(dist)]
    return b.transpose(2, 0, 1).reshape(N_KV_HEADS, GQA_GROUP, *dist.shape)


def bias_grouped(table, dist):
    tab = table.astype(jnp.float32).reshape(N_BUCKETS, N_KV_HEADS, GQA_GROUP)
    b = jax.vmap(lambda t, d: t[d], in_axes=(1, 1), out_axes=1)(tab, t5_bucket(dist))
    return b.transpose(0, 1, 4, 2, 3)


def masked_softmax(logits, mask):
    lg = jnp.where(mask, logits.astype(jnp.float32), -jnp.inf)
    m = jnp.max(lg, axis=-1, keepdims=True)
    m = jnp.where(jnp.isfinite(m), m, 0.0)
    e = jnp.where(mask, jnp.exp(lg - m), 0.0)
    return e / jnp.maximum(jnp.sum(e, axis=-1, keepdims=True), 1e-30)


def compress_tokens(t, pos_emb, w1, w2):
    B, G, S, dh = t.shape
    r = CMP_BLOCK // CMP_STRIDE
    n_chunks = S // CMP_STRIDE
    n_cmp = n_chunks - r + 1
    chunks = t.reshape(B, G, n_chunks, CMP_STRIDE, dh)
    blocks = jnp.concatenate([chunks[:, :, i:i + n_cmp] for i in range(r)], axis=3)
    blocks = (blocks + pos_emb).reshape(B, G, n_cmp, CMP_BLOCK * dh)
    return jax.nn.gelu(blocks @ w1) @ w2


gather_blocks = jax.vmap(jax.vmap(lambda blocks, ix: blocks[ix]))


def nsa_gmlp_mixer(h, w_in, cmp_pos_k, cmp_w1_k, cmp_w2_k, cmp_pos_v, cmp_w1_v, cmp_w2_v,
                   gmlp_ln_g, gmlp_ln_b, gmlp_w_s, gmlp_b_s, w_out, rel_bias):
    B, S, _ = h.shape
    G, R, dh = N_KV_HEADS, GQA_GROUP, HEAD_DIM
    proj = jnp.einsum("bsd,de->bse", h, w_in)
    sizes = [NSA_WIDTH] + [KV_WIDTH] * 6 + [N_GATES, GMLP_WIDTH]
    points = [sum(sizes[:i + 1]) for i in range(len(sizes))]
    q, k_c, v_c, k_s, v_s, k_w, v_w, gate_logits, u, v = jnp.split(proj, points, axis=-1)

    def heads_kv(t):
        return t.reshape(B, S, G, dh).transpose(0, 2, 1, 3)

    q = q.reshape(B, S, G, R, dh).transpose(0, 2, 3, 1, 4) * (HEAD_DIM ** -0.5)
    gates = jax.nn.sigmoid(gate_logits.astype(jnp.float32)).reshape(B, S, G, R, N_BRANCHES)
    gates = gates.transpose(0, 2, 3, 1, 4)

    kc = compress_tokens(heads_kv(k_c), cmp_pos_k, cmp_w1_k, cmp_w2_k)
    vc = compress_tokens(heads_kv(v_c), cmp_pos_v, cmp_w1_v, cmp_w2_v)
    n_cmp = kc.shape[2]
    cmp_start = jnp.arange(n_cmp, dtype=jnp.int32) * CMP_STRIDE
    cmp_end = cmp_start + (CMP_BLOCK - 1)

    n_slc = S // SLC_BLOCK
    ks_blocks = heads_kv(k_s).reshape(B, G, n_slc, SLC_BLOCK, dh)
    vs_blocks = heads_kv(v_s).reshape(B, G, n_slc, SLC_BLOCK, dh)
    slc_start = jnp.arange(n_slc, dtype=jnp.int32) * SLC_BLOCK
    ov = (jnp.minimum(cmp_start[:, None] + CMP_BLOCK, slc_start[None, :] + SLC_BLOCK)
          - jnp.maximum(cmp_start[:, None], slc_start[None, :]))
    overlap = jnp.maximum(ov, 0).astype(jnp.float32) / CMP_BLOCK
    k_sel = min(SLC_TOPK, n_slc)

    pad = ((0, 0), (0, 0), (WINDOW, 0), (0, 0))
    kw_pad = jnp.pad(heads_kv(k_w), pad)
    vw_pad = jnp.pad(heads_kv(v_w), pad)

    def query_block(qi):
        s0 = qi * Q_BLOCK
        tq = s0 + jnp.arange(Q_BLOCK, dtype=jnp.int32)
        qb = lax.dynamic_slice_in_dim(q, s0, Q_BLOCK, axis=3)
        gb = lax.dynamic_slice_in_dim(gates, s0, Q_BLOCK, axis=3)

        d_c = tq[:, None] - cmp_end[None, :]
        lg_c = jnp.einsum("bgrqd,bgkd->bgrqk", qb, kc).astype(jnp.float32) + bias_shared(rel_bias, d_c)
        p_c = masked_softmax(lg_c, d_c >= 0)
        o_c = jnp.einsum("bgrqk,bgkd->bgrqd", p_c.astype(vc.dtype), vc)

        imp = jnp.einsum("bgrqk,kn->bgqn", p_c, overlap)
        jq = (tq // SLC_BLOCK)[:, None]
        blk = jnp.arange(n_slc, dtype=jnp.int32)[None, :]
        forced = (blk == 0) | ((blk <= jq) & (blk > jq - N_LOCAL_BLOCKS))
        score = jnp.where(forced, jnp.inf, jnp.where(blk <= jq, imp, -jnp.inf))
        _, idx = lax.top_k(score, k_sel)
        kb = gather_blocks(ks_blocks, idx).reshape(B, G, Q_BLOCK, k_sel * SLC_BLOCK, dh)
        vb = gather_blocks(vs_blocks, idx).reshape(B, G, Q_BLOCK, k_sel * SLC_BLOCK, dh)
        key_pos = (idx[..., None] * SLC_BLOCK + jnp.arange(SLC_BLOCK, dtype=jnp.int32)
                   ).reshape(B, G, Q_BLOCK, k_sel * SLC_BLOCK)
        d_s = tq[None, None, :, None] - key_pos
        lg_s = jnp.einsum("bgrqd,bgqkd->bgrqk", qb, kb).astype(jnp.float32) + bias_grouped(rel_bias, d_s)
        p_s = masked_softmax(lg_s, (d_s >= 0)[:, :, None])
        o_s = jnp.einsum("bgrqk,bgqkd->bgrqd", p_s.astype(vb.dtype), vb)

        kwb = lax.dynamic_slice_in_dim(kw_pad, s0, WINDOW + Q_BLOCK, axis=2)
        vwb = lax.dynamic_slice_in_dim(vw_pad, s0, WINDOW + Q_BLOCK, axis=2)
        kpos = s0 - WINDOW + jnp.arange(WINDOW + Q_BLOCK, dtype=jnp.int32)
        d_w = tq[:, None] - kpos[None, :]
        mask_w = (d_w >= 0) & (d_w < WINDOW) & (kpos[None, :] >= 0)
        lg_w = jnp.einsum("bgrqd,bgkd->bgrqk", qb, kwb).astype(jnp.float32) + bias_shared(rel_bias, d_w)
        p_w = masked_softmax(lg_w, mask_w)
        o_w = jnp.einsum("bgrqk,bgkd->bgrqd", p_w.astype(vwb.dtype), vwb)

        o = gb[..., 0:1] * o_c + gb[..., 1:2] * o_s + gb[..., 2:3] * o_w
        return o.astype(h.dtype)

    o = lax.map(query_block, jnp.arange(S // Q_BLOCK, dtype=jnp.int32))
    nsa_out = o.transpose(1, 0, 4, 2, 3, 5).reshape(B, S, NSA_WIDTH)

    z_u = jax.nn.gelu(u)
    z_v = layer_norm(jax.nn.gelu(v), gmlp_ln_g, gmlp_ln_b)
    n_chunks = S // GMLP_CHUNK
    zv = z_v.reshape(B, n_chunks, GMLP_CHUNK, N_GMLP_GROUPS, GMLP_GROUP_DIM)
    causal = jnp.tril(jnp.ones((GMLP_CHUNK, GMLP_CHUNK), dtype=bool))
    w_s = jnp.where(causal, gmlp_w_s, 0.0).astype(zv.dtype)
    sv = jnp.einsum("gts,bnsgd->bntgd", w_s, zv) + gmlp_b_s.T[None, None, :, :, None]
    gm = (z_u.reshape(B, n_chunks, GMLP_CHUNK, N_GMLP_GROUPS, GMLP_GROUP_DIM) * sv).reshape(B, S, GMLP_WIDTH)

    mixed = jnp.concatenate([nsa_out, gm.astype(h.dtype)], axis=-1)
    return jnp.einsum("bse,ed->bsd", mixed, w_out)


def swiglu_ffn(h, w_gate_up, w_down):
    gate, up = jnp.split(jnp.einsum("bsd,df->bsf", h, w_gate_up), 2, axis=-1)
    return jnp.einsum("bsf,fd->bsd", jax.nn.silu(gate) * up, w_down)


def setup_inputs(seed: int = 0) -> dict:
    key = jax.random.key(seed)
    ks = jax.random.split(key, 20)
    f32 = jnp.float32
    L = DEPTH

    def nrm(k, shape, scale):
        return jax.random.normal(k, shape, f32) * scale

    def gain(k, shape):
        return 1.0 + 0.05 * jax.random.normal(k, shape, f32)

    return {
        "x": nrm(ks[0], (BATCH, SEQ, D_MODEL), 1.0),
        "rel_bias": nrm(ks[1], (N_BUCKETS, N_NSA_HEADS), 0.5),
        "norm_mix_pre": gain(ks[2], (L, D_MODEL)),
        "norm_mix_post": gain(ks[3], (L, D_MODEL)),
        "norm_ffn_pre": gain(ks[4], (L, D_MODEL)),
        "norm_ffn_post": gain(ks[5], (L, D_MODEL)),
        "w_in": nrm(ks[6], (L, D_MODEL, IN_WIDTH), D_MODEL ** -0.5),
        "cmp_pos_k": nrm(ks[7], (L, CMP_BLOCK, HEAD_DIM), 0.1),
        "cmp_w1_k": nrm(ks[8], (L, CMP_BLOCK * HEAD_DIM, CMP_HIDDEN), (CMP_BLOCK * HEAD_DIM) ** -0.5),
        "cmp_w2_k": nrm(ks[9], (L, CMP_HIDDEN, HEAD_DIM), CMP_HIDDEN ** -0.5),
        "cmp_pos_v": nrm(ks[10], (L, CMP_BLOCK, HEAD_DIM), 0.1),
        "cmp_w1_v": nrm(ks[11], (L, CMP_BLOCK * HEAD_DIM, CMP_HIDDEN), (CMP_BLOCK * HEAD_DIM) ** -0.5),
        "cmp_w2_v": nrm(ks[12], (L, CMP_HIDDEN, HEAD_DIM), CMP_HIDDEN ** -0.5),
        "gmlp_ln_g": gain(ks[13], (L, GMLP_WIDTH)),
        "gmlp_ln_b": nrm(ks[14], (L, GMLP_WIDTH), 0.02),
        "gmlp_w_s": nrm(ks[15], (L, N_GMLP_GROUPS, GMLP_CHUNK, GMLP_CHUNK), GMLP_CHUNK ** -0.5),
        "gmlp_b_s": gain(ks[16], (L, N_GMLP_GROUPS, GMLP_CHUNK)),
        "w_out": nrm(ks[17], (L, D_MODEL, D_MODEL), D_MODEL ** -0.5),
        "w_gate_up": nrm(ks[18], (L, D_MODEL, 2 * D_FF), D_MODEL ** -0.5),
        "w_down": nrm(ks[19], (L, D_FF, D_MODEL), D_FF ** -0.5),
    }


def reference(x, rel_bias, norm_mix_pre, norm_mix_post, norm_ffn_pre, norm_ffn_post, w_in,
              cmp_pos_k, cmp_w1_k, cmp_w2_k, cmp_pos_v, cmp_w1_v, cmp_w2_v,
              gmlp_ln_g, gmlp_ln_b, gmlp_w_s, gmlp_b_s, w_out, w_gate_up, w_down):
    h = x
    for l in range(DEPTH):
        mix = nsa_gmlp_mixer(rms_norm(h, norm_mix_pre[l]), w_in[l],
                             cmp_pos_k[l], cmp_w1_k[l], cmp_w2_k[l],
                             cmp_pos_v[l], cmp_w1_v[l], cmp_w2_v[l],
                             gmlp_ln_g[l], gmlp_ln_b[l], gmlp_w_s[l], gmlp_b_s[l],
                             w_out[l], rel_bias)
        h = h + rms_norm(mix, norm_mix_post[l])
        ffn = swiglu_ffn(rms_norm(h, norm_ffn_pre[l]), w_gate_up[l], w_down[l])
        h = h + rms_norm(ffn, norm_ffn_post[l])
    return h
```

```python
import numpy as np
import ml_dtypes
import concourse.bass as bass
import concourse.mybir as mybir
from concourse.bass_utils import run_bass_kernel_spmd

F32 = mybir.dt.float32
BF16 = mybir.dt.bfloat16
AF = mybir.ActivationFunctionType
ALU = mybir.AluOpType
AX = mybir.AxisListType
NPBF = ml_dtypes.bfloat16

NCORES = 8
NEG = -30000.0


class Tok:
    __slots__ = ("name", "w", "r")

    def __init__(self, name):
        self.name = name
        self.w = None
        self.r = {}


class Prog:
    ENG = ("pe", "act", "dve", "pool", "sp")
    _uid = 0

    def __init__(self, nc):
        Prog._uid += 1
        self.pfx = "p%d_" % Prog._uid
        self.nc = nc
        self.eng = {"pe": nc.tensor, "act": nc.scalar, "dve": nc.vector,
                    "pool": nc.gpsimd, "sp": nc.sync}
        self.q = {e: [] for e in self.ENG}
        self.cnt = {}
        self.sems = {}
        self.waited = {e: {} for e in self.ENG}
        self.ctx = []
        self.ndma = 0
        self.outs = []

    def _enter(self, cm):
        v = cm.__enter__()
        self.ctx.append(cm)
        return v

    def sb(self, name, shape, dt):
        return self._enter(self.nc.sbuf_tensor(self.pfx + name, list(shape), dt))

    def ps(self, name, shape, dt=F32):
        return self._enter(self.nc.psum_tensor(self.pfx + name, list(shape), dt))

    def sem(self, key):
        if key not in self.sems:
            self.sems[key] = self.nc.alloc_semaphore(name=self.pfx + "s_" + "_".join(str(k) for k in key))
            self.cnt[key] = 0
        return self.sems[key]

    def _deps(self, reads, writes):
        deps = {}

        def need(k, v):
            if v > deps.get(k, 0):
                deps[k] = v
        for b in reads:
            if b.w is not None:
                need(*b.w)
        for b in writes:
            if b.w is not None:
                need(*b.w)
            for k, v in b.r.items():
                need(k, v)
        return deps

    def _emit_waits(self, e, deps, skip_self_pe=True):
        for k, v in deps.items():
            if k == ("eng", "pe") and e == "pe":
                continue
            if self.waited[e].get(k, 0) >= v:
                continue
            self.waited[e][k] = v
            sem = self.sem(k)
            self.q[e].append(lambda E, sem=sem, v=v: E.wait_ge(sem, v))

    def op(self, e, fn, reads=(), writes=()):
        deps = self._deps(reads, writes)
        self._emit_waits(e, deps)
        key = ("eng", e)
        sem = self.sem(key)
        self.cnt[key] += 1
        v = self.cnt[key]
        self.q[e].append(lambda E, fn=fn, sem=sem: fn(E).then_inc(sem, 1))
        for b in reads:
            if b.r.get(key, 0) < v:
                b.r[key] = v
        for b in writes:
            b.w = (key, v)
            b.r = {}

    def dma(self, e, out_ap, in_ap, reads=(), writes=(), semname=None, **kw):
        deps = self._deps(reads, writes)
        self._emit_waits(e, deps)
        owner = (writes[0] if writes else reads[0]).name if semname is None else semname
        key = ("dma", owner)
        sem = self.sem(key)
        self.cnt[key] += 16
        v = self.cnt[key]
        self.q[e].append(lambda E, o=out_ap, i=in_ap, sem=sem, kw=kw: E.dma_start(out=o, in_=i, **kw).then_inc(sem, 16))
        for b in reads:
            if b.r.get(key, 0) < v:
                b.r[key] = v
        for b in writes:
            b.w = (key, v)
            b.r = {}
        self.ndma += 1

    def finish(self, out_toks, e="sp"):
        deps = {}
        for b in out_toks:
            if b.w is not None and b.w[1] > deps.get(b.w[0], 0):
                deps[b.w[0]] = b.w[1]
        self._emit_waits(e, deps)
        nc = self.nc
        with nc.Block() as block:
            @block.tensor
            def _(E):
                for f in self.q["pe"]:
                    f(E)

            @block.scalar
            def _(E):
                for f in self.q["act"]:
                    f(E)

            @block.vector
            def _(E):
                for f in self.q["dve"]:
                    f(E)

            @block.gpsimd
            def _(E):
                for f in self.q["pool"]:
                    f(E)

            @block.sync
            def _(E):
                for f in self.q["sp"]:
                    f(E)
        for cm in reversed(self.ctx):
            cm.__exit__(None, None, None)
        self.ctx = []
        nc.all_engine_barrier()
        nc.clear_and_free_semaphores(list(self.sems.values()))
        nc.all_engine_barrier()


D = 1024
S = 16384
B = 2
L_ = 4
DFF = 2816
INW = 2328
NTOK = B * S
TPC = NTOK // NCORES
NT = TPC // 128
EPS = 1e-6


def _dram(nc, name, shape, dt, out=False):
    return nc.dram_tensor(name, list(shape), dt, kind="ExternalOutput" if out else "ExternalInput").ap()


class IO:
    def __init__(self, nc, prefix="", override=None):
        self.nc = nc
        self.prefix = prefix
        self.override = override or {}

    def __call__(self, name, shape, dt, out=False):
        if name in self.override:
            return self.override[name]
        return _dram(self.nc, self.prefix + name, shape, dt, out)


def _new_nc():
    return bass.Bass("TRN2", target_bir_lowering=False)


def _scratch(nc, name, shape, dt):
    return nc.dram_tensor(name, list(shape), dt).ap()


CAST_CH = 4096


def build_cast(nch):
    nc = bass.Bass("TRN2", target_bir_lowering=False)
    x = _dram(nc, "x", [128, nch * CAST_CH], F32)
    y = _dram(nc, "y", [128, nch * CAST_CH], BF16, out=True)
    P = Prog(nc)
    st = [P.sb(f"st{i}", [128, CAST_CH], F32) for i in range(2)]
    ob = [P.sb(f"ob{i}", [128, CAST_CH], BF16) for i in range(2)]
    ts = [Tok(f"st{i}") for i in range(2)]
    to = [Tok(f"ob{i}") for i in range(2)]
    outs = []
    for c in range(nch):
        s = c % 2
        P.dma("sp", st[s][:], x[:, c * CAST_CH:(c + 1) * CAST_CH], writes=[ts[s]])
        if s == 0:
            P.op("dve", lambda E, s=s: E.tensor_copy(out=ob[s][:], in_=st[s][:]), reads=[ts[s]], writes=[to[s]])
        else:
            P.op("act", lambda E, s=s: E.activation(out=ob[s][:], in_=st[s][:], func=AF.Copy), reads=[ts[s]], writes=[to[s]])
        o = Tok(f"o{c}")
        P.dma("pool", y[:, c * CAST_CH:(c + 1) * CAST_CH], ob[s][:], reads=[to[s]], writes=[o], semname=f"ost{s}")
        outs.append(o)
    P.finish(outs)
    return nc


def emit_rmsnorm_to_bf16(P, src_ap, t_src, gain_sb, t_gain, sq, t_sq, st, t_st, dst_bf, t_dst, n=1024, extra_reads=()):
    P.op("act", lambda E: E.activation(out=sq, in_=src_ap, func=AF.Square), reads=[t_src] + list(extra_reads), writes=[t_sq])
    P.op("dve", lambda E: E.reduce_sum(out=st[:, 0:1], in_=sq, axis=AX.X), reads=[t_sq], writes=[t_st])
    P.op("act", lambda E: E.activation(out=st[:, 1:2], in_=st[:, 0:1], func=AF.Sqrt, bias=EPS, scale=1.0 / n), reads=[t_st], writes=[t_st])
    P.op("dve", lambda E: E.reciprocal(out=st[:, 2:3], in_=st[:, 1:2]), reads=[t_st], writes=[t_st])
    P.op("dve", lambda E: E.scalar_tensor_tensor(out=dst_bf, in0=src_ap, scalar=st[:, 2:3], in1=gain_sb, op0=ALU.mult, op1=ALU.mult),
         reads=[t_src, t_st, t_gain], writes=[t_dst])


def emit_A(nc, io, **opt):
    h = io("h", [TPC, D], F32)
    gpre = io("gpre", [128, D], F32)
    w = io("w", [128, 8, INW], BF16)
    lng = io("lng", [128, 512], F32)
    lnb = io("lnb", [128, 512], F32)
    wsT = io("wsT", [128, 8, 128], BF16)
    tri = io("tri", [128, 128], BF16)
    bsT = io("bsT", [128, 8], F32)
    ident_d = io("ident", [128, 128], BF16)
    q_o = io("q_o", [TPC, 512], BF16, out=True)
    kv_o = io("kv_o", [TPC, 768], BF16, out=True)
    g_o = io("g_o", [TPC, 24], F32, out=True)
    gm_o = io("gm_o", [TPC, 512], BF16, out=True)
    P = Prog(nc)
    wb = P.sb("wb", [128, 8, INW], BF16); t_wb = [Tok(f"wb{k}") for k in range(8)]
    for k in range(8):
        P.dma("sp" if k % 2 == 0 else "pool", wb[:, k, :], w[:, k, :], writes=[t_wb[k]])
    gp = P.sb("gp", [128, D], F32); t_gp = Tok("gp")
    P.dma("sp", gp[:], gpre[:, :], writes=[t_gp])
    lg = P.sb("lg", [128, 512], F32); t_lg = Tok("lg")
    lb = P.sb("lb", [128, 512], F32); t_lb = Tok("lb")
    P.dma("sp", lg[:], lng[:, :], writes=[t_lg])
    P.dma("sp", lb[:], lnb[:, :], writes=[t_lb])
    ws = P.sb("ws", [128, 8, 128], BF16); t_ws = Tok("ws")
    trs = P.sb("trs", [128, 128], BF16); t_tr = Tok("tr")
    P.dma("pool", ws[:], wsT[:, :, :], writes=[t_ws])
    P.dma("pool", trs[:], tri[:, :], writes=[t_tr])
    P.op("dve", lambda E: E.tensor_tensor(out=ws[:], in0=ws[:], in1=trs[:].unsqueeze(1).broadcast_to([128, 8, 128]), op=ALU.mult),
         reads=[t_tr, t_ws], writes=[t_ws])
    bs = P.sb("bs", [128, 8], F32); t_bs = Tok("bs")
    P.dma("pool", bs[:], bsT[:, :], writes=[t_bs])
    idn = P.sb("idn", [128, 128], BF16); t_id = Tok("idn")
    P.dma("pool", idn[:], ident_d[:, :], writes=[t_id])

    hf = [P.sb(f"hf{i}", [128, D], F32) for i in range(2)]; t_hf = [Tok(f"hf{i}") for i in range(2)]
    sq = P.sb("sq", [128, D], F32); t_sq = Tok("sq")
    st = P.sb("st", [128, 16], F32); t_st = Tok("st")
    hn = P.sb("hn", [128, D], BF16); t_hn = Tok("hn")
    hnT = P.sb("hnT", [128, 8, 128], BF16); t_hnT = Tok("hnT")
    TP = P.ps("TP", [128, 8, 128], BF16); t_TP = Tok("TP")
    PJ = [P.ps(f"PJ{i}", [128, 512]) for i in range(5)]; t_PJ = [Tok(f"PJ{i}") for i in range(5)]
    SV = P.ps("SV", [128, 512]); t_SV = Tok("SV")
    qo = [P.sb(f"qo{i}", [128, 512], BF16) for i in range(2)]; t_qo = [Tok(f"qo{i}") for i in range(2)]
    kvo = [P.sb(f"kvo{i}", [128, 768], BF16) for i in range(2)]; t_kvo = [Tok(f"kvo{i}") for i in range(2)]
    go = [P.sb(f"go{i}", [128, 24], F32) for i in range(2)]; t_go = [Tok(f"go{i}") for i in range(2)]
    gmo = [P.sb(f"gmo{i}", [128, 512], BF16) for i in range(2)]; t_gmo = [Tok(f"gmo{i}") for i in range(2)]
    zu = P.sb("zu", [128, 512], F32); t_zu = Tok("zu")
    gv = P.sb("gv", [128, 512], F32); t_gv = Tok("gv")
    zv = P.sb("zv", [128, 512], F32); t_zv = Tok("zv")
    zvb = P.sb("zvb", [128, 512], BF16); t_zvb = Tok("zvb")
    widths = [512, 512, 280, 512, 512]
    offs = [0, 512, 1024, 1304, 1816]
    outs = []
    for i in range(NT):
        s = i % 2
        P.dma("sp", hf[s][:], h[i * 128:(i + 1) * 128, :], writes=[t_hf[s]])
        emit_rmsnorm_to_bf16(P, hf[s][:], t_hf[s], gp[:], t_gp, sq[:], t_sq, st, t_st, hn[:], t_hn)
        for k in range(8):
            P.op("pe", lambda E, k=k: E.transpose(out=TP[:, k, :], in_=hn[:, k * 128:(k + 1) * 128], identity=idn[:]),
                 reads=[t_hn, t_id], writes=[t_TP])
        P.op("act", lambda E: E.activation(out=hnT[:], in_=TP[:], func=AF.Copy), reads=[t_TP], writes=[t_hnT])
        for c in range(5):
            for k in range(8):
                P.op("pe", lambda E, c=c, k=k: E.matmul(PJ[c][:, 0:widths[c]], lhsT=hnT[:, k, :], rhs=wb[:, k, offs[c]:offs[c] + widths[c]],
                                                         start=(k == 0), stop=(k == 7)),
                     reads=[t_hnT, t_wb[k]], writes=[t_PJ[c]])
        P.op("act", lambda E, s=s: E.activation(out=qo[s][:], in_=PJ[0][:], func=AF.Copy, scale=0.125), reads=[t_PJ[0]], writes=[t_qo[s]])
        P.op("dve", lambda E, s=s: E.tensor_copy(out=kvo[s][:, 0:512], in_=PJ[1][:]), reads=[t_PJ[1]], writes=[t_kvo[s]])
        P.op("dve", lambda E, s=s: E.tensor_copy(out=kvo[s][:, 512:768], in_=PJ[2][:, 0:256]), reads=[t_PJ[2], t_kvo[s]], writes=[t_kvo[s]])
        P.op("act", lambda E, s=s: E.activation(out=go[s][:], in_=PJ[2][:, 256:280], func=AF.Sigmoid), reads=[t_PJ[2]], writes=[t_go[s]])
        P.op("act", lambda E: E.activation(out=zu[:], in_=PJ[3][:], func=AF.Gelu_apprx_tanh), reads=[t_PJ[3]], writes=[t_zu])
        P.op("act", lambda E: E.activation(out=gv[:], in_=PJ[4][:], func=AF.Gelu_apprx_tanh), reads=[t_PJ[4]], writes=[t_gv])
        P.op("dve", lambda E: E.reduce_sum(out=st[:, 4:5], in_=gv[:], axis=AX.X), reads=[t_gv], writes=[t_st])
        P.op("act", lambda E: E.activation(out=sq[:, 0:512], in_=gv[:], func=AF.Square), reads=[t_gv], writes=[t_sq])
        P.op("dve", lambda E: E.reduce_sum(out=st[:, 5:6], in_=sq[:, 0:512], axis=AX.X), reads=[t_sq], writes=[t_st])
        P.op("dve", lambda E: E.tensor_scalar(out=st[:, 6:7], in0=st[:, 4:5], scalar1=1.0 / 512, scalar2=None, op0=ALU.mult), reads=[t_st], writes=[t_st])
        P.op("dve", lambda E: E.tensor_tensor(out=st[:, 7:8], in0=st[:, 6:7], in1=st[:, 6:7], op=ALU.mult), reads=[t_st], writes=[t_st])
        P.op("dve", lambda E: E.scalar_tensor_tensor(out=st[:, 8:9], in0=st[:, 5:6], scalar=1.0 / 512, in1=st[:, 7:8], op0=ALU.mult, op1=ALU.subtract),
             reads=[t_st], writes=[t_st])
        P.op("act", lambda E: E.activation(out=st[:, 9:10], in_=st[:, 8:9], func=AF.Sqrt, bias=1e-5, scale=1.0), reads=[t_st], writes=[t_st])
        P.op("dve", lambda E: E.reciprocal(out=st[:, 10:11], in_=st[:, 9:10]), reads=[t_st], writes=[t_st])
        P.op("dve", lambda E: E.tensor_scalar(out=zv[:], in0=gv[:], scalar1=st[:, 6:7], scalar2=st[:, 10:11], op0=ALU.subtract, op1=ALU.mult),
             reads=[t_gv, t_st], writes=[t_zv])
        P.op("dve", lambda E: E.tensor_tensor(out=zv[:], in0=zv[:], in1=lg[:], op=ALU.mult), reads=[t_zv, t_lg], writes=[t_zv])
        P.op("dve", lambda E: E.tensor_tensor(out=zvb[:], in0=zv[:], in1=lb[:], op=ALU.add), reads=[t_zv, t_lb], writes=[t_zvb])
        for g in range(8):
            P.op("pe", lambda E, g=g: E.matmul(SV[:, g * 64:(g + 1) * 64], lhsT=ws[:, g, :], rhs=zvb[:, g * 64:(g + 1) * 64],
                                               start=(g == 0), stop=(g == 7), skip_group_check=True),
                 reads=[t_ws, t_zvb], writes=[t_SV])
        for g in range(8):
            P.op("dve", lambda E, g=g, s=s: E.scalar_tensor_tensor(out=gmo[s][:, g * 64:(g + 1) * 64], in0=SV[:, g * 64:(g + 1) * 64],
                                                                  scalar=bs[:, g:g + 1], in1=zu[:, g * 64:(g + 1) * 64], op0=ALU.add, op1=ALU.mult),
                 reads=[t_SV, t_bs, t_zu] + ([t_gmo[s]] if g else []), writes=[t_gmo[s]])
        for nm, dst, src, tk in (("q", q_o, qo, t_qo), ("kv", kv_o, kvo, t_kvo), ("g", g_o, go, t_go), ("gm", gm_o, gmo, t_gmo)):
            o = Tok(f"o_{nm}{i}")
            P.dma("pool", dst[i * 128:(i + 1) * 128, :], src[s][:], reads=[tk[s]], writes=[o], semname=f"o_{nm}{s}")
            outs.append(o)
    P.finish(outs)
    nc.all_engine_barrier()


def build_A():
    nc = _new_nc()
    emit_A(nc, IO(nc))
    return nc


_PROGS = {}


def _prog(name, builder, *a):
    key = (name,) + tuple(a)
    if key not in _PROGS:
        _PROGS[key] = builder(*a)
    return _PROGS[key]


def _run(nc, in_maps):
    import time as _t
    t0 = _t.time()
    res = run_bass_kernel_spmd(nc, in_maps, core_ids=list(range(NCORES)))
    print("[launch] %.1fs" % (_t.time() - t0), flush=True)
    return res.results


def _raw(a):
    a = np.asarray(a)
    return a.view(np.uint16) if a.dtype == NPBF else a


def _unraw(a, bf):
    return a.view(NPBF) if bf else a


def _const(v, bf):
    return np.array(v, np.float32).astype(NPBF).view(np.uint16) if bf else np.float64(v)


def _tr(a, axes):
    if a.dtype == NPBF:
        return np.ascontiguousarray(a.view(np.uint16).transpose(axes)).view(NPBF)
    return np.ascontiguousarray(a.transpose(axes))


def _rep(v, n=128):
    return np.ascontiguousarray(np.broadcast_to(np.asarray(v, np.float32)[None, :], (n, v.shape[0])))


def cast_all(arrs):
    flat = np.concatenate([np.asarray(a, np.float32).ravel() for a in arrs])
    per = NCORES * 128 * CAST_CH
    nch = -(-flat.size // per)
    pad = np.zeros(nch * per, np.float32)
    pad[:flat.size] = flat
    x = pad.reshape(NCORES, 128, nch * CAST_CH)
    nc = _prog("cast", build_cast, nch)
    res = _run(nc, [{"x": x[r]} for r in range(NCORES)])
    y = np.concatenate([np.asarray(res[r]["y"]).reshape(-1) for r in range(NCORES)])
    out = []
    o = 0
    for a in arrs:
        out.append(y[o:o + a.size].reshape(a.shape))
        o += a.size
    return out


IN_PERM = np.concatenate([np.arange(0, 1280), np.arange(1280, 1304), np.arange(1304, 2328)])


def _consts():
    c = {}
    c["ident"] = np.eye(128, dtype=np.float32).astype(NPBF)
    s = np.arange(128)
    c["tri"] = (s[:, None] <= s[None, :]).astype(np.float32).astype(NPBF)
    return c


def run_A(h, l, W, C):
    nc = _prog("A", build_A)
    wb = np.ascontiguousarray(W["w_in"][l].reshape(8, 128, INW).transpose(1, 0, 2))
    wsT = np.ascontiguousarray(W["gmlp_w_s"][l].transpose(2, 0, 1))
    common = {"gpre": _rep(W["norm_mix_pre"][l]), "w": wb, "lng": _rep(W["gmlp_ln_g"][l]), "lnb": _rep(W["gmlp_ln_b"][l]),
              "wsT": wsT, "tri": C["tri"], "bsT": np.ascontiguousarray(W["gmlp_b_s"][l].T), "ident": C["ident"]}
    maps = [dict(common, h=h[r * TPC:(r + 1) * TPC]) for r in range(NCORES)]
    res = _run(nc, maps)
    cat = lambda k: np.concatenate([np.asarray(res[r][k]) for r in range(NCORES)], 0)
    return cat("q_o"), cat("kv_o"), cat("g_o"), cat("gm_o")


def emit_C1(nc, io, **opt):
    h = io("h", [TPC, D], F32)
    mixT = io("mixT", [128, NT, 8, 128], BF16)
    w = io("w", [128, 8, D], BF16)
    gpost = io("gpost", [128, D], F32)
    h_o = io("h_o", [TPC, D], F32, out=True)
    P = Prog(nc)
    wb = P.sb("wb", [128, 8, D], BF16); t_wb = Tok("wb")
    P.dma("pool", wb[:], w[:, :, :], writes=[t_wb])
    gp = P.sb("gp", [128, D], F32); t_gp = Tok("gp")
    P.dma("pool", gp[:], gpost[:, :], writes=[t_gp])
    hf = [P.sb(f"hf{i}", [128, D], F32) for i in range(2)]; t_hf = [Tok(f"hf{i}") for i in range(2)]
    mx = [P.sb(f"mx{i}", [128, 8, 128], BF16) for i in range(2)]; t_mx = [Tok(f"mx{i}") for i in range(2)]
    sq = P.sb("sq", [128, D], F32); t_sq = Tok("sq")
    st = P.sb("st", [128, 4], F32); t_st = Tok("st")
    tn = P.sb("tn", [128, D], F32); t_tn = Tok("tn")
    ho = [P.sb(f"ho{i}", [128, D], F32) for i in range(2)]; t_ho = [Tok(f"ho{i}") for i in range(2)]
    MX = [P.ps(f"MX{i}", [128, 2, 512]) for i in range(2)]; t_MX = [Tok(f"MX{i}") for i in range(2)]
    outs = []
    for i in range(NT):
        s = i % 2
        P.dma("sp", hf[s][:], h[i * 128:(i + 1) * 128, :], writes=[t_hf[s]])
        P.dma("sp", mx[s][:], mixT[:, i, :, :], writes=[t_mx[s]])
        for hh in range(2):
            for k in range(8):
                P.op("pe", lambda E, hh=hh, k=k, s=s: E.matmul(MX[s][:, hh, :], lhsT=mx[s][:, k, :], rhs=wb[:, k, hh * 512:(hh + 1) * 512],
                                                              start=(k == 0), stop=(k == 7)),
                     reads=[t_mx[s], t_wb], writes=[t_MX[s]])
        mxv = MX[s][:].rearrange("p a b -> p (a b)")
        emit_rmsnorm_to_bf16(P, mxv, t_MX[s], gp[:], t_gp, sq[:], t_sq, st, t_st, tn[:], t_tn)
        P.op("dve", lambda E, s=s: E.tensor_tensor(out=ho[s][:], in0=hf[s][:], in1=tn[:], op=ALU.add), reads=[t_hf[s], t_tn], writes=[t_ho[s]])
        o = Tok(f"o{i}")
        P.dma("pool", h_o[i * 128:(i + 1) * 128, :], ho[s][:], reads=[t_ho[s]], writes=[o], semname=f"o{s}")
        outs.append(o)
    P.finish(outs)
    nc.all_engine_barrier()


def build_C1():
    nc = _new_nc()
    emit_C1(nc, IO(nc))
    return nc


NF = DFF // 128


def emit_C2(nc, io, **opt):
    h = io("h", [TPC, D], F32)
    wgu = io("wgu", [128, 8, 2 * DFF], BF16)
    wdn = io("wdn", [128, NF, D], BF16)
    gpre = io("gpre", [128, D], F32)
    gpost = io("gpost", [128, D], F32)
    ident_d = io("ident", [128, 128], BF16)
    h_o = io("h_o", [TPC, D], F32, out=True)
    P = Prog(nc)
    wg = P.sb("wg", [128, 8, 2 * DFF], BF16); t_wg = [Tok(f"wg{k}") for k in range(8)]
    for k in range(8):
        P.dma("sp" if k % 2 == 0 else "pool", wg[:, k, :], wgu[:, k, :], writes=[t_wg[k]])
    wd = P.sb("wd", [128, NF, D], BF16); t_wd = Tok("wd")
    P.dma("pool", wd[:], wdn[:, :, :], writes=[t_wd])
    g1 = P.sb("g1", [128, D], F32); t_g1 = Tok("g1")
    g2 = P.sb("g2", [128, D], F32); t_g2 = Tok("g2")
    P.dma("sp", g1[:], gpre[:, :], writes=[t_g1])
    P.dma("sp", g2[:], gpost[:, :], writes=[t_g2])
    idn = P.sb("idn", [128, 128], BF16); t_id = Tok("idn")
    P.dma("sp", idn[:], ident_d[:, :], writes=[t_id])
    hf = [P.sb(f"hf{i}", [128, D], F32) for i in range(4)]; t_hf = [Tok(f"hf{i}") for i in range(4)]
    sq = P.sb("sq", [128, D], F32); t_sq = Tok("sq")
    st = P.sb("st", [128, 4], F32); t_st = Tok("st")
    hn = P.sb("hn", [128, D], BF16); t_hn = Tok("hn")
    hnT = P.sb("hnT", [128, 8, 512], BF16); t_hnT = Tok("hnT")
    sg = [P.sb(f"sg{i}", [128, 512], F32) for i in range(2)]; t_sg = [Tok(f"sg{i}") for i in range(2)]
    actT = P.sb("actT", [128, NF, 512], BF16); t_act = [Tok(f"act{f}") for f in range(NF)]
    ho = [P.sb(f"ho{i}", [128, D], F32) for i in range(2)]; t_ho = [Tok(f"ho{i}") for i in range(2)]
    TP = P.ps("TP", [128, 8, 128], BF16); t_TP = Tok("TP")
    GP = [P.ps(f"GP{i}", [128, 512]) for i in range(2)]; t_GP = [Tok(f"GP{i}") for i in range(2)]
    UP = [P.ps(f"UP{i}", [128, 512]) for i in range(2)]; t_UP = [Tok(f"UP{i}") for i in range(2)]
    DN = P.ps("DN", [128, 2, 512]); t_DN = Tok("DN")
    outs = []
    for gi in range(NT // 4):
        for j in range(4):
            i = gi * 4 + j
            P.dma("sp", hf[j][:], h[i * 128:(i + 1) * 128, :], writes=[t_hf[j]])
            emit_rmsnorm_to_bf16(P, hf[j][:], t_hf[j], g1[:], t_g1, sq[:], t_sq, st, t_st, hn[:], t_hn)
            for k in range(8):
                P.op("pe", lambda E, k=k: E.transpose(out=TP[:, k, :], in_=hn[:, k * 128:(k + 1) * 128], identity=idn[:]),
                     reads=[t_hn, t_id], writes=[t_TP])
            P.op("act", lambda E, j=j: E.activation(out=hnT[:, :, j * 128:(j + 1) * 128], in_=TP[:], func=AF.Copy), reads=[t_TP], writes=[t_hnT])
        for f in range(NF):
            s = f % 2
            for k in range(8):
                P.op("pe", lambda E, f=f, k=k, s=s: E.matmul(GP[s][:], lhsT=wg[:, k, f * 128:(f + 1) * 128], rhs=hnT[:, k, :], start=(k == 0), stop=(k == 7)),
                     reads=[t_wg[k], t_hnT], writes=[t_GP[s]])
            for k in range(8):
                P.op("pe", lambda E, f=f, k=k, s=s: E.matmul(UP[s][:], lhsT=wg[:, k, DFF + f * 128:DFF + (f + 1) * 128], rhs=hnT[:, k, :], start=(k == 0), stop=(k == 7)),
                     reads=[t_wg[k], t_hnT], writes=[t_UP[s]])
            P.op("act", lambda E, s=s: E.activation(out=sg[s][:], in_=GP[s][:], func=AF.Silu), reads=[t_GP[s]], writes=[t_sg[s]])
            P.op("dve", lambda E, s=s, f=f: E.tensor_tensor(out=actT[:, f, :], in0=sg[s][:], in1=UP[s][:], op=ALU.mult), reads=[t_sg[s], t_UP[s]], writes=[t_act[f]])
        for j in range(4):
            i = gi * 4 + j
            s = i % 2
            for hh in range(2):
                for f in range(NF):
                    P.op("pe", lambda E, hh=hh, f=f, j=j: E.matmul(DN[:, hh, :], lhsT=actT[:, f, j * 128:(j + 1) * 128], rhs=wd[:, f, hh * 512:(hh + 1) * 512],
                                                                   start=(f == 0), stop=(f == NF - 1)),
                         reads=[t_act[f], t_wd], writes=[t_DN])
            dnv = DN[:].rearrange("p a b -> p (a b)")
            emit_rmsnorm_to_bf16(P, dnv, t_DN, g2[:], t_g2, sq[:], t_sq, st, t_st, ho[s][:], t_ho[s])
            P.op("dve", lambda E, s=s, j=j: E.tensor_tensor(out=ho[s][:], in0=hf[j][:], in1=ho[s][:], op=ALU.add), reads=[t_hf[j], t_ho[s]], writes=[t_ho[s]])
            o = Tok(f"o{i}")
            P.dma("pool", h_o[i * 128:(i + 1) * 128, :], ho[s][:], reads=[t_ho[s]], writes=[o], semname=f"o{s}")
            outs.append(o)
    P.finish(outs)
    nc.all_engine_barrier()


def build_C2():
    nc = _new_nc()
    emit_C2(nc, IO(nc))
    return nc


def run_C1(h, mixed, l, W):
    nc = _prog("C1", build_C1)
    wb = np.ascontiguousarray(W["w_out"][l].reshape(8, 128, D).transpose(1, 0, 2))
    gp = _rep(W["norm_mix_post"][l])
    maps = []
    for r in range(NCORES):
        m = mixed[r * TPC:(r + 1) * TPC].reshape(NT, 128, 8, 128)
        maps.append({"h": h[r * TPC:(r + 1) * TPC], "mixT": _tr(m, (3, 0, 2, 1)), "w": wb, "gpost": gp})
    res = _run(nc, maps)
    return np.concatenate([np.asarray(res[r]["h_o"]) for r in range(NCORES)], 0)


def run_C2(h, l, W, C):
    nc = _prog("C2", build_C2)
    wgu = np.ascontiguousarray(W["w_gate_up"][l].reshape(8, 128, 2 * DFF).transpose(1, 0, 2))
    wdn = np.ascontiguousarray(W["w_down"][l].reshape(NF, 128, D).transpose(1, 0, 2))
    common = {"wgu": wgu, "wdn": wdn, "gpre": _rep(W["norm_ffn_pre"][l]), "gpost": _rep(W["norm_ffn_post"][l]), "ident": C["ident"]}
    res = _run(nc, [dict(common, h=h[r * TPC:(r + 1) * TPC]) for r in range(NCORES)])
    return np.concatenate([np.asarray(res[r]["h_o"]) for r in range(NCORES)], 0)


def emit_Bc(nc, io, **opt):
    xT = io("xT", [64, S + 16], BF16)
    w1 = io("w1", [64, 32, 256], BF16)
    posT = io("posT", [64, 32], BF16)
    w2 = io("w2", [128, 2, 64], BF16)
    ccT_o = io("ccT", [64, 1024], BF16, out=True)
    cc_o = io("cc", [1024, 64], BF16, out=True)
    P = Prog(nc)
    xs = P.sb("xs", [64, S + 16], BF16); t_xs = Tok("xs")
    P.dma("sp", xs[:], xT[:, :], writes=[t_xs])
    w1s = P.sb("w1s", [64, 32, 256], BF16); t_w1 = Tok("w1")
    P.dma("pool", w1s[:], w1[:, :, :], writes=[t_w1])
    ps_ = P.sb("ps_", [64, 32], BF16); t_ps = Tok("ps")
    P.dma("pool", ps_[:], posT[:, :], writes=[t_ps])
    w2s = P.sb("w2s", [128, 2, 64], BF16); t_w2 = Tok("w2")
    P.dma("pool", w2s[:], w2[:, :, :], writes=[t_w2])
    PB = P.ps("PB", [128, 2]); t_PB = Tok("PB")
    H = [P.ps(f"H{c}", [128, 512]) for c in range(2)]; t_H = [Tok(f"H{c}") for c in range(2)]
    CT = P.ps("CT", [64, 512]); t_CT = Tok("CT")
    CC = P.ps("CC", [128, 4, 64]); t_CC = Tok("CC")
    posb = P.sb("posb", [128, 2], F32); t_posb = Tok("posb")
    GT = P.sb("GT", [128, 2, 512], BF16); t_GT = Tok("GT")
    cts = P.sb("cts", [64, 512], BF16); t_cts = Tok("cts")
    ccs = P.sb("ccs", [128, 4, 64], BF16); t_ccs = Tok("ccs")
    for c in range(2):
        for l in range(32):
            P.op("pe", lambda E, c=c, l=l: E.matmul(PB[:, c:c + 1], lhsT=w1s[:, l, c * 128:(c + 1) * 128], rhs=ps_[:, l:l + 1], start=(l == 0), stop=(l == 31)),
                 reads=[t_w1, t_ps], writes=[t_PB])
    P.op("dve", lambda E: E.tensor_copy(out=posb[:], in_=PB[:]), reads=[t_PB], writes=[t_posb])
    outs = []
    cc_v = cc_o.rearrange("(t p) d -> p t d", p=128)
    for j in range(2):
        base = j * 8192
        for c in range(2):
            for l in range(32):
                b0 = base + (16 if l >= 16 else 0)
                rhs = xs[:, b0:b0 + 8192].rearrange("p (i s) -> p i s", s=16)[:, :, l % 16]
                P.op("pe", lambda E, c=c, l=l, rhs=rhs: E.matmul(H[c][:], lhsT=w1s[:, l, c * 128:(c + 1) * 128], rhs=rhs, start=(l == 0), stop=(l == 31)),
                     reads=[t_w1, t_xs], writes=[t_H[c]])
            P.op("act", lambda E, c=c: E.activation(out=GT[:, c, :], in_=H[c][:], func=AF.Gelu_apprx_tanh, bias=posb[:, c:c + 1]),
                 reads=[t_H[c], t_posb], writes=[t_GT])
        for c in range(2):
            P.op("pe", lambda E, c=c: E.matmul(CT[:], lhsT=w2s[:, c, :], rhs=GT[:, c, :], start=(c == 0), stop=(c == 1)), reads=[t_w2, t_GT], writes=[t_CT])
        P.op("dve", lambda E: E.tensor_copy(out=cts[:], in_=CT[:]), reads=[t_CT], writes=[t_cts])
        o = Tok(f"oT{j}")
        P.dma("sp", ccT_o[:, j * 512:(j + 1) * 512], cts[:], reads=[t_cts], writes=[o], semname="oT")
        outs.append(o)
        for bt in range(4):
            for c in range(2):
                P.op("pe", lambda E, c=c, bt=bt: E.matmul(CC[:, bt, :], lhsT=GT[:, c, bt * 128:(bt + 1) * 128], rhs=w2s[:, c, :], start=(c == 0), stop=(c == 1)),
                     reads=[t_w2, t_GT], writes=[t_CC])
        P.op("dve", lambda E: E.tensor_copy(out=ccs[:], in_=CC[:]), reads=[t_CC], writes=[t_ccs])
        o = Tok(f"oc{j}")
        P.dma("sp", cc_v[:, j * 4:(j + 1) * 4, :], ccs[:], reads=[t_ccs], writes=[o], semname="oc")
        outs.append(o)
    P.finish(outs)
    nc.all_engine_barrier()


def build_Bc():
    nc = _new_nc()
    emit_Bc(nc, IO(nc))
    return nc


NPAIR = 64
NADD = 25


def emit_B(nc, io, **opt):
    qd = io("qd", [65, NPAIR, 1024], BF16)
    kTs_d = io("kTs", [65, S], BF16)
    vs_d = io("vs", [128, 128, 65], BF16)
    kTw_d = io("kTw", [65, S], BF16)
    vw_d = io("vw", [128, 128, 65], BF16)
    cmp_src = opt.get("cmp_src")
    if cmp_src is None:
        kcT_d = io("kcT", [65, 1024], BF16)
        vc_d = io("vc", [128, 8, 65], BF16)
    else:
        onesrow_d = io("onesrow", [1, 1024], BF16)
        vc1_d = io("vc1", [128, 8, 1], BF16)
        zrow_d = io("zrow", [1, 65], BF16)
    gts_d = io("gts", [128, 128, 6], F32)
    ov_d = io("ov", [128, 8, 256], BF16)
    ex_d = io("expall", [128, 64, 128], BF16)
    id_d = io("ident", [128, 128], BF16)
    tv_d = io("tv", [128, 512], F32)
    tf_d = io("tf", [128, 512], F32)
    tva_d = io("tvalid", [128, 512], F32)
    bg_d = io("bg", [128, NADD, 512], BF16)
    fb_d = io("farb", [128, 2, 4], BF16)
    o_d = io("o", [S, 128], BF16, out=True)
    P = Prog(nc)

    def load(name, d, shape, dt, eng):
        t = P.sb(name, shape, dt)
        tk = Tok(name)
        P.dma(eng, t[:], d, writes=[tk])
        return t, tk
    if cmp_src is None:
        kcT, t_kcT = load("kcT_s", kcT_d[:, :], [65, 1024], BF16, "sp")
        vc, t_vc = load("vc_s", vc_d[:, :, :], [128, 8, 65], BF16, "sp")
    else:
        kcT = P.sb("kcT_s", [65, 1024], BF16); t_kcT = Tok("kcT_s")
        vc = P.sb("vc_s", [128, 8, 65], BF16); t_vc = Tok("vc_s")
        P.dma("sp", kcT[0:64, :], cmp_src[0][:, :], writes=[t_kcT])
        P.dma("sp", kcT[64:65, :], onesrow_d[:, :], writes=[t_kcT])
        P.dma("sp", vc[:, :, 0:64], cmp_src[1].rearrange("(t p) d -> p t d", p=128), writes=[t_vc])
        P.dma("sp", vc[:, :, 64:65], vc1_d[:, :, :], writes=[t_vc], allow_slow_non_contiguous=True)
        P.dma("sp", vc[127:128, 7, :], zrow_d[:, :], writes=[t_vc])
    ov, t_ov = load("ov_s", ov_d[:, :, :], [128, 8, 256], BF16, "sp")
    idn, t_id = load("id_s", id_d[:, :], [128, 128], BF16, "sp")
    add, t_add = load("add_s", bg_d[:, :, :], [128, NADD, 512], BF16, "sp")
    fb, t_fb = load("fb_s", fb_d[:, :, :], [128, 2, 4], BF16, "sp")
    gts, t_gts = load("gts_s", gts_d[:, :, :], [128, 128, 6], F32, "sp")
    tv, t_tv = load("tv_s", tv_d[:, :], [128, 512], F32, "sp")
    tf, t_tf = load("tf_s", tf_d[:, :], [128, 512], F32, "sp")
    tva, t_tva = load("tva_s", tva_d[:, :], [128, 512], F32, "sp")
    kTw, t_kTw = load("kTw_s", kTw_d[:, :], [65, S], BF16, "pool")
    vw, t_vw = load("vw_s", vw_d[:, :, :], [128, 128, 65], BF16, "pool")
    kTs, t_kTs = load("kTs_s", kTs_d[:, :], [65, S], BF16, "sp")
    vs, t_vs = load("vs_s", vs_d[:, :, :], [128, 128, 65], BF16, "pool")
    ex, t_ex = load("ex_s", ex_d[:, :, :], [128, 64, 128], BF16, "pool")
    for i in range(NADD):
        which = 0 if i < 17 else 1
        a3 = add[:, i, :].rearrange("p (x q) -> p x q", q=128)
        P.op("dve", lambda E, a3=a3, which=which: E.tensor_tensor(out=a3, in0=a3, in1=fb[:, which, :].unsqueeze(2).broadcast_to([128, 4, 128]), op=ALU.subtract),
             reads=[t_fb, t_add], writes=[t_add])

    qs = [P.sb(f"qs{i}", [65, 2, 2, 2, 128], BF16) for i in range(2)]; t_qs = [Tok(f"qs{i}") for i in range(2)]
    es = [P.sb(f"es{i}", [128, 512], BF16) for i in range(3)]; t_es = [Tok(f"es{i}") for i in range(3)]
    Lb = [P.ps(f"L{i}", [128, 512]) for i in range(2)]; t_L = [Tok(f"L{i}") for i in range(2)]
    OC = P.ps("OC", [128, 512]); t_OC = Tok("OC")
    IMP = [P.ps(f"IMP{i}", [128, 512]) for i in range(2)]; t_IMP = Tok("IMP")
    OS = P.ps("OS", [128, 512]); t_OS = Tok("OS")
    OW = P.ps("OW", [128, 512]); t_OW = Tok("OW")
    TPn = P.ps("TPn", [128, 2, 2, 128], BF16); t_TPn = Tok("TPn")
    nsT = P.sb("nsT", [128, 2, 2, 128], BF16); t_nsT = Tok("nsT")
    negsel = [P.sb(f"negsel{t}", [128, 256], BF16) for t in range(2)]; t_neg = [Tok(f"neg{t}") for t in range(2)]
    acc = [P.sb(f"acc{t}", [128, 2, 64], F32) for t in range(2)]; t_acc = [Tok(f"acc{t}") for t in range(2)]
    ob = [P.sb(f"ob{t}", [128, 128], BF16) for t in range(2)]; t_ob = [Tok(f"ob{t}") for t in range(2)]
    rz = P.sb("rz", [128, 8], F32); t_rz = Tok("rz")
    rz2 = P.sb("rz2", [128, 8], F32); t_rz2 = Tok("rz2")
    imp = P.sb("imp", [128, 256], F32); t_imp = Tok("imp")
    score = P.sb("score", [128, 256], F32); t_score = Tok("score")
    scr = P.sb("scr", [128, 256], F32); t_scr = Tok("scr")
    sel = P.sb("sel", [128, 256], F32); t_sel = Tok("sel")
    m8 = P.sb("m8", [128, 16], F32); t_m8 = Tok("m8")
    OCv = OC[:, 0:260].rearrange("p (r c) -> p r c", c=65)
    OSv = OS[:, 0:260].rearrange("p (r c) -> p r c", c=65)
    OWv = OW[:, 0:260].rearrange("p (r c) -> p r c", c=65)
    outs = []
    state = {"pending": None, "li": 0, "ei": 0}

    def flush():
        p = state["pending"]
        if p is not None:
            p["exp"]()
            p["pv"]()
            if p.get("after"):
                p["after"]()
        state["pending"] = None

    def push(qk, pv, after=None):
        li = state["li"]; state["li"] = 1 - li
        ei = state["ei"]; state["ei"] = (ei + 1) % 3
        qk(li)
        flush()

        def ex_(li=li, ei=ei):
            P.op("act", lambda E: E.activation(out=es[ei][:], in_=Lb[li][:], func=AF.Exp), reads=[t_L[li]], writes=[t_es[ei]])
        state["pending"] = {"exp": ex_, "pv": (lambda ei=ei: pv(ei)), "after": after}

    def mm(out, lhsT, rhs, start, stop, reads, writes):
        P.op("pe", lambda E: E.matmul(out, lhsT=lhsT, rhs=rhs, start=start, stop=stop, skip_group_check=True), reads=reads, writes=writes)

    for n in range(NPAIR):
        s = n % 2
        P.dma("sp", qs[s][:].rearrange("p a b c d -> p (a b c d)"), qd[:, n, :], writes=[t_qs[s]])
        for t in range(2):
            gt = 2 * n + t
            nkc = gt // 16 + 1
            for kc in range(nkc):
                d = gt - 16 * kc

                def qk(li, kc=kc, d=d, t=t, s=s):
                    mm(Lb[li][:, 0:256], kcT[0:65, kc * 128:(kc + 1) * 128], qs[s][0:65, 0, t, :, :], True, False, [t_kcT, t_qs[s]], [t_L[li]])
                    mm(Lb[li][:, 256:512], kcT[0:65, kc * 128:(kc + 1) * 128], qs[s][0:65, 1, t, :, :], False, d > 16, [t_kcT, t_qs[s]], [t_L[li]])
                    if d <= 16:
                        mm(Lb[li][:], idn[:], add[:, d, :], False, True, [t_id, t_add], [t_L[li]])

                def pv(ei, kc=kc, nkc=nkc):
                    for r in range(4):
                        mm(OC[:, r * 65:(r + 1) * 65], es[ei][:, r * 128:(r + 1) * 128], vc[:, kc, :], kc == 0 and r == 0, kc == nkc - 1, [t_es[ei], t_vc], [t_OC])
                    for r in range(4):
                        mm(IMP[r // 2][:, (r % 2) * 256:(r % 2) * 256 + 256], es[ei][:, r * 128:(r + 1) * 128], ov[:, kc, :], kc == 0 and r % 2 == 0, kc == nkc - 1,
                           [t_es[ei], t_ov], [t_IMP])

                def after(t=t, gt=gt):
                    off = 256 - 2 * gt
                    g3 = gts[:, gt, :].rearrange("p (h b) -> p h b", b=3)
                    P.op("dve", lambda E: E.tensor_scalar(out=rz[:, 0:4], in0=OCv[:, :, 64], scalar1=1e-30, scalar2=None, op0=ALU.max), reads=[t_OC], writes=[t_rz])
                    P.op("dve", lambda E: E.reciprocal(out=rz[:, 0:4], in_=rz[:, 0:4]), reads=[t_rz], writes=[t_rz])
                    P.op("dve", lambda E: E.tensor_tensor(out=rz[:, 4:6], in0=rz[:, 0:2], in1=g3[:, :, 0], op=ALU.mult), reads=[t_rz, t_gts], writes=[t_rz])
                    for hh in range(2):
                        P.op("dve", lambda E, hh=hh: E.tensor_scalar(out=acc[t][:, hh, :], in0=OCv[:, hh, 0:64], scalar1=rz[:, 4 + hh:5 + hh], scalar2=None, op0=ALU.mult),
                             reads=[t_OC, t_rz], writes=[t_acc[t]])
                    P.op("dve", lambda E: E.tensor_scalar(out=imp[:], in0=IMP[0][:, 0:256], scalar1=rz[:, 0:1], scalar2=None, op0=ALU.mult), reads=[t_IMP, t_rz], writes=[t_imp])
                    for r in range(1, 4):
                        P.op("dve", lambda E, r=r: E.scalar_tensor_tensor(out=imp[:], in0=IMP[r // 2][:, (r % 2) * 256:(r % 2) * 256 + 256], scalar=rz[:, r:r + 1], in1=imp[:],
                                                                         op0=ALU.mult, op1=ALU.add), reads=[t_IMP, t_rz, t_imp], writes=[t_imp])
                    P.op("dve", lambda E: E.tensor_tensor(out=score[:], in0=imp[:], in1=tv[:, off:off + 256], op=ALU.mult), reads=[t_imp, t_tv], writes=[t_score])
                    P.op("dve", lambda E: E.tensor_tensor(out=score[:], in0=score[:], in1=tf[:, off:off + 256], op=ALU.add), reads=[t_score, t_tf], writes=[t_score])
                    P.op("dve", lambda E: E.memset(score[:, 0:1], 3.0e9), reads=[t_score], writes=[t_score])
                    P.op("dve", lambda E: E.max(out=m8[:, 0:8], in_=score[:]), reads=[t_score], writes=[t_m8])
                    P.op("dve", lambda E: E.match_replace(out=scr[:], in_to_replace=m8[:, 0:8], in_values=score[:], imm_value=-3.0e9), reads=[t_score, t_m8], writes=[t_scr])
                    P.op("dve", lambda E: E.max(out=m8[:, 8:16], in_=scr[:]), reads=[t_scr, t_m8], writes=[t_m8])
                    P.op("dve", lambda E: E.tensor_scalar(out=sel[:], in0=score[:], scalar1=m8[:, 15:16], scalar2=None, op0=ALU.is_ge), reads=[t_score, t_m8], writes=[t_sel])
                    P.op("dve", lambda E: E.tensor_tensor(out=sel[:], in0=sel[:], in1=tva[:, off:off + 256], op=ALU.mult), reads=[t_sel, t_tva], writes=[t_sel])
                    P.op("dve", lambda E: E.tensor_scalar(out=negsel[t][:], in0=sel[:], scalar1=1.0, scalar2=-NEG, op0=ALU.subtract, op1=ALU.mult), reads=[t_sel], writes=[t_neg[t]])

                push(qk, pv, after if kc == nkc - 1 else None)
        wlist = [jp for jp in range(6) if 2 * n + jp - 4 >= 0]
        widx = {0: 20, 1: 21, 3: 22, 4: 23, 5: 24}
        for jp in wlist:
            kt = 2 * n + jp - 4

            def qk(li, kt=kt, jp=jp, s=s):
                mm(Lb[li][:], kTw[0:65, kt * 128:(kt + 1) * 128], qs[s][0:65, 0, :, :, :], True, jp == 2, [t_kTw, t_qs[s]], [t_L[li]])
                if jp != 2:
                    mm(Lb[li][:], idn[:], add[:, widx[jp], :], False, True, [t_id, t_add], [t_L[li]])

            def pv(ei, kt=kt, jp=jp, first=(jp == wlist[0]), last=(jp == wlist[-1])):
                for x in range(4):
                    mm(OW[:, x * 65:(x + 1) * 65], es[ei][:, x * 128:(x + 1) * 128], vw[:, kt, :], first and x == 0, last, [t_es[ei], t_vw], [t_OW])
            push(qk, pv)
        nks = 2 * n + 2
        for kt in range(nks):
            def qk(li, kt=kt, n=n, s=s):
                if kt == 0:
                    for t in range(2):
                        for c in range(2):
                            P.op("pe", lambda E, t=t, c=c: E.transpose(out=TPn[:, t, c, :], in_=negsel[t][:, c * 128:(c + 1) * 128], identity=idn[:]),
                                 reads=[t_neg[t], t_id], writes=[t_TPn])
                    P.op("act", lambda E: E.activation(out=nsT[:], in_=TPn[:], func=AF.Copy), reads=[t_TPn], writes=[t_nsT])
                mm(Lb[li][:], kTs[0:65, kt * 128:(kt + 1) * 128], qs[s][0:65, 0, :, :, :], True, False, [t_kTs, t_qs[s]], [t_L[li]])
                mm(Lb[li][:].rearrange("p (t h q) -> p t h q", t=2, h=2), ex[:, kt % 64, :],
                   nsT[:, :, kt // 64, :].unsqueeze(2).broadcast_to([128, 2, 2, 128]), False, kt < 2 * n - 1, [t_ex, t_nsT], [t_L[li]])
                if kt >= 2 * n - 1:
                    mm(Lb[li][:], idn[:], add[:, 17 + kt - (2 * n - 1), :], False, True, [t_id, t_add], [t_L[li]])

            def pv(ei, kt=kt, nks=nks):
                for x in range(4):
                    mm(OS[:, x * 65:(x + 1) * 65], es[ei][:, x * 128:(x + 1) * 128], vs[:, kt, :], kt == 0 and x == 0, kt == nks - 1, [t_es[ei], t_vs], [t_OS])

            def epi(t, gt):
                g3 = gts[:, gt, :].rearrange("p (h b) -> p h b", b=3)
                for (Ov, tO, o0, br) in ((OSv, t_OS, 0, 1), (OWv, t_OW, 4, 2)):
                    P.op("dve", lambda E, Ov=Ov, o0=o0: E.tensor_scalar(out=rz2[:, o0:o0 + 2], in0=Ov[:, 2 * t:2 * t + 2, 64], scalar1=1e-30, scalar2=None, op0=ALU.max),
                         reads=[tO], writes=[t_rz2])
                    P.op("dve", lambda E, o0=o0: E.reciprocal(out=rz2[:, o0:o0 + 2], in_=rz2[:, o0:o0 + 2]), reads=[t_rz2], writes=[t_rz2])
                    P.op("dve", lambda E, o0=o0, br=br: E.tensor_tensor(out=rz2[:, o0 + 2:o0 + 4], in0=rz2[:, o0:o0 + 2], in1=g3[:, :, br], op=ALU.mult),
                         reads=[t_rz2, t_gts], writes=[t_rz2])
                for hh in range(2):
                    P.op("dve", lambda E, hh=hh: E.scalar_tensor_tensor(out=acc[t][:, hh, :], in0=OSv[:, 2 * t + hh, 0:64], scalar=rz2[:, 2 + hh:3 + hh], in1=acc[t][:, hh, :],
                                                                       op0=ALU.mult, op1=ALU.add), reads=[t_OS, t_rz2, t_acc[t]], writes=[t_acc[t]])
                for hh in range(2):
                    P.op("dve", lambda E, hh=hh: E.scalar_tensor_tensor(out=ob[t][:, hh * 64:(hh + 1) * 64], in0=OWv[:, 2 * t + hh, 0:64], scalar=rz2[:, 6 + hh:7 + hh], in1=acc[t][:, hh, :],
                                                                       op0=ALU.mult, op1=ALU.add), reads=[t_OW, t_rz2, t_acc[t]] + ([t_ob[t]] if hh else []), writes=[t_ob[t]])
                o = Tok(f"o{gt}")
                P.dma("pool", o_d[gt * 128:(gt + 1) * 128, :], ob[t][:], reads=[t_ob[t]], writes=[o], semname=f"o{t}")
                outs.append(o)

            def after(n=n):
                for t in range(2):
                    epi(t, 2 * n + t)
            push(qk, pv, after if kt == nks - 1 else None)
    flush()
    P.finish(outs)
    nc.all_engine_barrier()


def build_B():
    nc = _new_nc()
    emit_B(nc, IO(nc))
    return nc


def _bucket(dist):
    n = np.maximum(dist, 0)
    nf = np.maximum(n, 16).astype(np.float32)
    large = 16 + (np.log(nf / np.float32(16)) / np.float32(np.log(8.0)) * np.float32(16)).astype(np.int32)
    return np.where(n < 16, n, np.minimum(large, 31))


def _attn_consts():
    c = {}
    i = np.arange(1024)[:, None]
    nn = np.arange(256)[None, :]
    ovl = np.maximum(np.minimum(16 * i + 32, 64 * nn + 64) - np.maximum(16 * i, 64 * nn), 0).astype(np.float32) / 32.0
    ovl[1023] = 0
    c["ov"] = np.ascontiguousarray(ovl.reshape(8, 128, 256).transpose(1, 0, 2)).astype(NPBF)
    p = np.arange(128)[:, None, None]
    v = np.arange(64)[None, :, None]
    key = np.arange(128)[None, None, :]
    c["expall"] = (p == 2 * v + key // 64).astype(np.float32).astype(NPBF)
    q = np.arange(128)[:, None]
    rel = np.arange(512)[None, :] - 256
    jl = q // 64
    c["tv"] = (rel <= jl - 2).astype(np.float32)
    tfm = np.zeros((128, 512), np.float32)
    tfm[rel > jl + 0 * rel] = -1.0e9
    tfm[(rel == jl) & (rel == rel)] = 2.0e9
    tfm[rel == jl - 1] = 1.0e9
    c["tf"] = tfm
    c["tvalid"] = (rel <= jl + 0 * rel).astype(np.float32)
    r = np.arange(128)[:, None, None]
    qq = np.arange(128)[None, None, :]
    dist = np.zeros((NADD, 128, 4, 128), np.int64)
    vis = np.zeros((NADD, 128, 4, 128), bool)
    for d in range(17):
        dd = 128 * d + qq - 16 * r - 31 + np.zeros((1, 4, 1), np.int64)
        dist[d] = dd
        vis[d] = dd >= 0
    tt = np.array([0, 0, 1, 1])[None, :, None]
    for j in range(3):
        dd = 128 * (tt + 1 - j) + qq - r
        dist[17 + j] = dd
        vis[17 + j] = dd >= 0
    for ix, jp in enumerate([0, 1, 3, 4, 5]):
        dd = 128 * (tt + 4 - jp) + qq - r
        dist[20 + ix] = dd
        vis[20 + ix] = (dd >= 0) & (dd < 512)
    c["bucket"] = _bucket(dist)
    c["vis"] = vis
    return c


def run_Bc(kv, l, W):
    nc = _prog("Bc", build_Bc)
    kvu = kv.view(np.uint16)
    maps = []
    for r in range(NCORES):
        b, g, kvi = r // 4, (r // 2) % 2, r % 2
        xT = np.zeros((64, S + 16), np.uint16)
        xT[:, :S] = kvu[b * S:(b + 1) * S, kvi * 128 + g * 64: kvi * 128 + (g + 1) * 64].T
        sfx = "k" if kvi == 0 else "v"
        w1 = _tr(W["cmp_w1_" + sfx][l].reshape(32, 64, 256), (1, 0, 2))
        posT = _tr(W["cmp_pos_" + sfx][l], (1, 0))
        w2 = _tr(W["cmp_w2_" + sfx][l].reshape(2, 128, 64), (1, 0, 2))
        maps.append({"xT": xT.view(NPBF), "w1": w1, "posT": posT, "w2": w2})
    res = _run(nc, maps)
    out = {}
    for r in range(NCORES):
        b, g, kvi = r // 4, (r // 2) % 2, r % 2
        out[(b, g, kvi)] = (np.asarray(res[r]["ccT"]), np.asarray(res[r]["cc"]))
    return out


def run_B(q, kv, gates, cmp, W, AC):
    nc = _prog("B", build_B)
    qu = q.view(np.uint16)
    kvu = kv.view(np.uint16)
    T = W["rel_bias"]
    Tu = T.view(np.uint16)
    one = np.array(1.0, np.float32).astype(NPBF).view(np.uint16)
    negu = np.array(NEG, np.float32).astype(NPBF).view(np.uint16)
    maps = []
    for r in range(NCORES):
        b, g, hp = r // 4, (r // 2) % 2, r % 2
        mine = [g * 4 + hp * 2, g * 4 + hp * 2 + 1]
        oth = [g * 4 + (1 - hp) * 2, g * 4 + (1 - hp) * 2 + 1]
        heads = mine + oth
        Qb = qu[b * S:(b + 1) * S].reshape(NPAIR, 2, 128, 8, 64)[:, :, :, heads, :]
        Qb = Qb.reshape(NPAIR, 2, 128, 2, 2, 64)
        qd = np.empty((65, NPAIR, 2, 2, 2, 128), np.uint16)
        qd[:64] = Qb.transpose(5, 0, 3, 1, 4, 2)
        fbh = Tu[31, heads].reshape(2, 2)
        qd[64] = fbh[None, :, None, :, None]
        KVb = kvu[b * S:(b + 1) * S]

        def kT(col):
            a = np.empty((65, S), np.uint16)
            a[:64] = KVb[:, col + g * 64: col + (g + 1) * 64].T
            a[64] = one
            return a.view(NPBF)

        def vaug(col):
            a = np.empty((128, 128, 65), np.uint16)
            a[:, :, :64] = KVb[:, col + g * 64: col + (g + 1) * 64].reshape(128, 128, 64).transpose(1, 0, 2)
            a[:, :, 64] = one
            return a.view(NPBF)
        ccT = cmp[(b, g, 0)][0].view(np.uint16)
        cc = cmp[(b, g, 1)][1].view(np.uint16)
        kcT = np.empty((65, 1024), np.uint16)
        kcT[:64] = ccT
        kcT[64] = one
        vc = np.empty((128, 8, 65), np.uint16)
        vc[:, :, :64] = cc.reshape(8, 128, 64).transpose(1, 0, 2)
        vc[:, :, 64] = one
        vc[127, 7, :] = 0
        gsel = gates[b * S:(b + 1) * S, g * 12 + hp * 6: g * 12 + hp * 6 + 6]
        gts = np.ascontiguousarray(gsel.reshape(128, 128, 6).transpose(1, 0, 2))
        bg = np.empty((NADD, 128, 4, 128), np.uint16)
        hx_c = np.array(heads)
        hx_p = np.array(mine + mine)
        for i in range(NADD):
            hx = hx_c if i < 17 else hx_p
            g_ = Tu[AC["bucket"][i], hx[None, :, None]]
            bg[i] = np.where(AC["vis"][i], g_, negu)
        farb = np.empty((128, 2, 4), np.uint16)
        farb[:, 0, :] = Tu[31, hx_c][None, :]
        farb[:, 1, :] = Tu[31, hx_p][None, :]
        maps.append({"qd": qd.reshape(65, NPAIR, 1024).view(NPBF), "kTs": kT(256), "vs": vaug(384), "kTw": kT(512), "vw": vaug(640),
                     "kcT": kcT.view(NPBF), "vc": vc.view(NPBF), "gts": gts, "ov": AC["ov"], "expall": AC["expall"], "ident": AC["ident"],
                     "tv": AC["tv"], "tf": AC["tf"], "tvalid": AC["tvalid"],
                     "bg": np.ascontiguousarray(bg.transpose(1, 0, 2, 3)).reshape(128, NADD, 512).view(NPBF), "farb": farb.view(NPBF)})
    res = _run(nc, maps)
    nsa = np.empty((NTOK, 512), np.uint16)
    for r in range(NCORES):
        b, g, hp = r // 4, (r // 2) % 2, r % 2
        c0 = (g * 4 + hp * 2) * 64
        nsa[b * S:(b + 1) * S, c0:c0 + 128] = np.asarray(res[r]["o"]).view(np.uint16)
    return nsa.view(NPBF)


CAST_NAMES = ["w_in", "w_out", "w_gate_up", "w_down", "cmp_w1_k", "cmp_w1_v", "cmp_w2_k", "cmp_w2_v", "cmp_pos_k", "cmp_pos_v", "gmlp_w_s", "rel_bias"]


def build_X(first, last):
    nc = _new_nc()
    h2 = None
    if not first:
        h1 = _scratch(nc, "h1s", [TPC, D], F32)
        h2 = _dram(nc, "h2", [TPC, D], F32, out=True)
        emit_C1(nc, IO(nc, "c1_", {"h_o": h1}))
        emit_C2(nc, IO(nc, "c2_", {"h": h1, "h_o": h2}))
    if not last:
        emit_A(nc, IO(nc, "a_", {} if first else {"h": h2}))
    return nc


def build_Y():
    nc = _new_nc()
    kc_ccT = _scratch(nc, "kc_ccT", [64, 1024], BF16)
    kc_cc = _scratch(nc, "kc_cc", [1024, 64], BF16)
    vc_ccT = _scratch(nc, "vc_ccT", [64, 1024], BF16)
    vc_cc = _scratch(nc, "vc_cc", [1024, 64], BF16)
    emit_Bc(nc, IO(nc, "k_", {"ccT": kc_ccT, "cc": kc_cc}))
    emit_Bc(nc, IO(nc, "v_", {"ccT": vc_ccT, "cc": vc_cc}))
    emit_B(nc, IO(nc, ""), cmp_src=(kc_ccT, vc_cc))
    return nc


def run_X(l, first, last, h, mixed, W, C):
    nc = _prog("X", build_X, first, last)
    common = {}
    if not first:
        lp = l - 1
        common.update({"c1_w": _tr(W["w_out"][lp].reshape(8, 128, D), (1, 0, 2)), "c1_gpost": _rep(W["norm_mix_post"][lp]),
                       "c2_wgu": _tr(W["w_gate_up"][lp].reshape(8, 128, 2 * DFF), (1, 0, 2)),
                       "c2_wdn": _tr(W["w_down"][lp].reshape(NF, 128, D), (1, 0, 2)),
                       "c2_gpre": _rep(W["norm_ffn_pre"][lp]), "c2_gpost": _rep(W["norm_ffn_post"][lp]), "c2_ident": C["ident"]})
    if not last:
        common.update({"a_gpre": _rep(W["norm_mix_pre"][l]), "a_w": _tr(W["w_in"][l].reshape(8, 128, INW), (1, 0, 2)),
                       "a_lng": _rep(W["gmlp_ln_g"][l]), "a_lnb": _rep(W["gmlp_ln_b"][l]), "a_wsT": _tr(W["gmlp_w_s"][l], (2, 0, 1)),
                       "a_tri": C["tri"], "a_bsT": np.ascontiguousarray(W["gmlp_b_s"][l].T), "a_ident": C["ident"]})
    maps = []
    for r in range(NCORES):
        m = dict(common)
        if first:
            m["a_h"] = h[r * TPC:(r + 1) * TPC]
        else:
            m["c1_h"] = h[r * TPC:(r + 1) * TPC]
            mm_ = mixed[r * TPC:(r + 1) * TPC].reshape(NT, 128, 8, 128)
            m["c1_mixT"] = _tr(mm_, (3, 0, 2, 1))
        maps.append(m)
    res = _run(nc, maps)
    cat = lambda k: np.concatenate([np.asarray(res[r][k]) for r in range(NCORES)], 0)
    h2 = None if first else cat("h2")
    if last:
        return h2, None
    return h2, (cat("a_q_o"), cat("a_kv_o"), cat("a_g_o"), cat("a_gm_o"))


def run_Y(l, q, kv, gates, W, AC):
    nc = _prog("Y", build_Y)
    bf = np.asarray(q).dtype == NPBF
    qu = _raw(q)
    kvu = _raw(kv)
    Tu = _raw(W["rel_bias"])
    rdt = qu.dtype
    one = _const(1.0, bf)
    negu = _const(NEG, bf)
    cw = {}
    for sfx in ("k", "v"):
        cw[sfx] = {"w1": _tr(W["cmp_w1_" + sfx][l].reshape(32, 64, 256), (1, 0, 2)), "posT": _tr(W["cmp_pos_" + sfx][l], (1, 0)),
                   "w2": _tr(W["cmp_w2_" + sfx][l].reshape(2, 128, 64), (1, 0, 2))}
    onesrow = _unraw(np.full((1, 1024), one, rdt), bf)
    vc1 = np.full((128, 8, 1), one, rdt)
    vc1[127, 7, 0] = 0
    vc1 = _unraw(vc1, bf)
    zrow = _unraw(np.zeros((1, 65), rdt), bf)
    maps = []
    for r in range(NCORES):
        b, g, hp = r // 4, (r // 2) % 2, r % 2
        mine = [g * 4 + hp * 2, g * 4 + hp * 2 + 1]
        oth = [g * 4 + (1 - hp) * 2, g * 4 + (1 - hp) * 2 + 1]
        heads = mine + oth
        Qb = qu[b * S:(b + 1) * S].reshape(NPAIR, 2, 128, 8, 64)[:, :, :, heads, :]
        Qb = Qb.reshape(NPAIR, 2, 128, 2, 2, 64)
        qd = np.empty((65, NPAIR, 2, 2, 2, 128), rdt)
        qd[:64] = Qb.transpose(5, 0, 3, 1, 4, 2)
        qd[64] = Tu[31, heads].reshape(2, 2)[None, :, None, :, None]
        KVb = kvu[b * S:(b + 1) * S]

        def kT(col):
            a = np.empty((65, S), rdt)
            a[:64] = KVb[:, col + g * 64: col + (g + 1) * 64].T
            a[64] = one
            return _unraw(a, bf)

        def vaug(col):
            a = np.empty((128, 128, 65), rdt)
            a[:, :, :64] = KVb[:, col + g * 64: col + (g + 1) * 64].reshape(128, 128, 64).transpose(1, 0, 2)
            a[:, :, 64] = one
            return _unraw(a, bf)

        def xT(col):
            a = np.zeros((64, S + 16), rdt)
            a[:, :S] = KVb[:, col + g * 64: col + (g + 1) * 64].T
            return _unraw(a, bf)
        gsel = np.asarray(gates)[b * S:(b + 1) * S, g * 12 + hp * 6: g * 12 + hp * 6 + 6]
        gts = np.ascontiguousarray(gsel.reshape(128, 128, 6).transpose(1, 0, 2))
        bg = np.empty((NADD, 128, 4, 128), Tu.dtype)
        hx_c = np.array(heads)
        hx_p = np.array(mine + mine)
        for i in range(NADD):
            hx = hx_c if i < 17 else hx_p
            bg[i] = np.where(AC["vis"][i], Tu[AC["bucket"][i], hx[None, :, None]], negu)
        farb = np.empty((128, 2, 4), Tu.dtype)
        farb[:, 0, :] = Tu[31, hx_c][None, :]
        farb[:, 1, :] = Tu[31, hx_p][None, :]
        tbf = np.asarray(W["rel_bias"]).dtype == NPBF
        m = {"qd": _unraw(qd.reshape(65, NPAIR, 1024), bf), "kTs": kT(256), "vs": vaug(384), "kTw": kT(512), "vw": vaug(640),
             "gts": gts, "ov": AC["ov"], "expall": AC["expall"], "ident": AC["ident"],
             "tv": AC["tv"], "tf": AC["tf"], "tvalid": AC["tvalid"],
             "bg": _unraw(np.ascontiguousarray(bg.transpose(1, 0, 2, 3)).reshape(128, NADD, 512), tbf), "farb": _unraw(farb, tbf),
             "onesrow": onesrow, "vc1": vc1, "zrow": zrow,
             "k_xT": xT(0), "v_xT": xT(128)}
        for sfx in ("k", "v"):
            for nm in ("w1", "posT", "w2"):
                m[sfx + "_" + nm] = cw[sfx][nm]
        maps.append(m)
    res = _run(nc, maps)
    o0 = _raw(res[0]["o"])
    obf = np.asarray(res[0]["o"]).dtype == NPBF
    nsa = np.empty((NTOK, 512), o0.dtype)
    for r in range(NCORES):
        b, g, hp = r // 4, (r // 2) % 2, r % 2
        c0 = (g * 4 + hp * 2) * 64
        nsa[b * S:(b + 1) * S, c0:c0 + 128] = _raw(res[r]["o"])
    return _unraw(nsa, obf)


def kernel(**inputs):
    inp = {k: np.asarray(v) for k, v in inputs.items()}
    W = dict(inp)
    for n, a in zip(CAST_NAMES, cast_all([inp[n] for n in CAST_NAMES])):
        W[n] = a
    C = _consts()
    AC = _attn_consts()
    AC["ident"] = C["ident"]
    h = np.ascontiguousarray(inp["x"].reshape(NTOK, D).astype(np.float32))
    mixed = None
    for l in range(L_ + 1):
        h2, aout = run_X(l, l == 0, l == L_, h, mixed, W, C)
        if h2 is not None:
            h = h2
        if aout is None:
            break
        q, kv, gates, gm = aout
        nsa = run_Y(l, q, kv, gates, W, AC)
        if np.asarray(nsa).dtype == NPBF and np.asarray(gm).dtype == NPBF:
            mixed = np.concatenate([_raw(nsa), _raw(gm)], axis=1).view(NPBF)
        else:
            mixed = np.concatenate([np.asarray(nsa, np.float64), np.asarray(gm, np.float64)], axis=1)
    return np.asarray(h).reshape(B, S, D).astype(np.float32)
```

```python
import numpy as np
import ml_dtypes
import concourse.bass as bass
import concourse.mybir as mybir
from concourse.bass_utils import run_bass_kernel_spmd

F32 = mybir.dt.float32
BF16 = mybir.dt.bfloat16
AF = mybir.ActivationFunctionType
ALU = mybir.AluOpType
AX = mybir.AxisListType
NPBF = ml_dtypes.bfloat16

NCORES = 8
NEG = -30000.0


class Tok:
    __slots__ = ("name", "w", "r")

    def __init__(self, name):
        self.name = name
        self.w = None
        self.r = {}


class Prog:
    ENG = ("pe", "act", "dve", "pool", "sp")
    _uid = 0

    def __init__(self, nc):
        Prog._uid += 1
        self.pfx = "p%d_" % Prog._uid
        self.nc = nc
        self.eng = {"pe": nc.tensor, "act": nc.scalar, "dve": nc.vector,
                    "pool": nc.gpsimd, "sp": nc.sync}
        self.q = {e: [] for e in self.ENG}
        self.cnt = {}
        self.sems = {}
        self.waited = {e: {} for e in self.ENG}
        self.ctx = []
        self.ndma = 0
        self.outs = []

    def _enter(self, cm):
        v = cm.__enter__()
        self.ctx.append(cm)
        return v

    def sb(self, name, shape, dt):
        return self._enter(self.nc.sbuf_tensor(self.pfx + name, list(shape), dt))

    def ps(self, name, shape, dt=F32):
        return self._enter(self.nc.psum_tensor(self.pfx + name, list(shape), dt))

    def sem(self, key):
        if key not in self.sems:
            self.sems[key] = self.nc.alloc_semaphore(name=self.pfx + "s_" + "_".join(str(k) for k in key))
            self.cnt[key] = 0
        return self.sems[key]

    def _deps(self, reads, writes):
        deps = {}

        def need(k, v):
            if v > deps.get(k, 0):
                deps[k] = v
        for b in reads:
            if b.w is not None:
                need(*b.w)
        for b in writes:
            if b.w is not None:
                need(*b.w)
            for k, v in b.r.items():
                need(k, v)
        return deps

    def _emit_waits(self, e, deps, skip_self_pe=True):
        for k, v in deps.items():
            if k == ("eng", "pe") and e == "pe":
                continue
            if self.waited[e].get(k, 0) >= v:
                continue
            self.waited[e][k] = v
            sem = self.sem(k)
            self.q[e].append(lambda E, sem=sem, v=v: E.wait_ge(sem, v))

    def op(self, e, fn, reads=(), writes=()):
        deps = self._deps(reads, writes)
        self._emit_waits(e, deps)
        key = ("eng", e)
        sem = self.sem(key)
        self.cnt[key] += 1
        v = self.cnt[key]
        self.q[e].append(lambda E, fn=fn, sem=sem: fn(E).then_inc(sem, 1))
        for b in reads:
            if b.r.get(key, 0) < v:
                b.r[key] = v
        for b in writes:
            b.w = (key, v)
            b.r = {}

    def dma(self, e, out_ap, in_ap, reads=(), writes=(), semname=None, **kw):
        deps = self._deps(reads, writes)
        self._emit_waits(e, deps)
        owner = (writes[0] if writes else reads[0]).name if semname is None else semname
        key = ("dma", owner)
        sem = self.sem(key)
        self.cnt[key] += 16
        v = self.cnt[key]
        self.q[e].append(lambda E, o=out_ap, i=in_ap, sem=sem, kw=kw: E.dma_start(out=o, in_=i, **kw).then_inc(sem, 16))
        for b in reads:
            if b.r.get(key, 0) < v:
                b.r[key] = v
        for b in writes:
            b.w = (key, v)
            b.r = {}
        self.ndma += 1

    def finish(self, out_toks, e="sp"):
        deps = {}
        for b in out_toks:
            if b.w is not None and b.w[1] > deps.get(b.w[0], 0):
                deps[b.w[0]] = b.w[1]
        self._emit_waits(e, deps)
        nc = self.nc
        with nc.Block() as block:
            @block.tensor
            def _(E):
                for f in self.q["pe"]:
                    f(E)

            @block.scalar
            def _(E):
                for f in self.q["act"]:
                    f(E)

            @block.vector
            def _(E):
                for f in self.q["dve"]:
                    f(E)

            @block.gpsimd
            def _(E):
                for f in self.q["pool"]:
                    f(E)

            @block.sync
            def _(E):
                for f in self.q["sp"]:
                    f(E)
        for cm in reversed(self.ctx):
            cm.__exit__(None, None, None)
        self.ctx = []
        nc.all_engine_barrier()
        nc.clear_and_free_semaphores(list(self.sems.values()))
        nc.all_engine_barrier()


D = 1024
S = 16384
B = 2
L_ = 4
DFF = 2816
INW = 2328
NTOK = B * S
TPC = NTOK // NCORES
NT = TPC // 128
EPS = 1e-6


def _dram(nc, name, shape, dt, out=False):
    return nc.dram_tensor(name, list(shape), dt, kind="ExternalOutput" if out else "ExternalInput").ap()


class IO:
    def __init__(self, nc, prefix="", override=None):
        self.nc = nc
        self.prefix = prefix
        self.override = override or {}

    def __call__(self, name, shape, dt, out=False):
        if name in self.override:
            return self.override[name]
        return _dram(self.nc, self.prefix + name, shape, dt, out)


def _new_nc():
    return bass.Bass("TRN2", target_bir_lowering=False)


def _scratch(nc, name, shape, dt):
    return nc.dram_tensor(name, list(shape), dt).ap()


CAST_CH = 4096


def build_cast(nch):
    nc = bass.Bass("TRN2", target_bir_lowering=False)
    x = _dram(nc, "x", [128, nch * CAST_CH], F32)
    y = _dram(nc, "y", [128, nch * CAST_CH], BF16, out=True)
    P = Prog(nc)
    st = [P.sb(f"st{i}", [128, CAST_CH], F32) for i in range(2)]
    ob = [P.sb(f"ob{i}", [128, CAST_CH], BF16) for i in range(2)]
    ts = [Tok(f"st{i}") for i in range(2)]
    to = [Tok(f"ob{i}") for i in range(2)]
    outs = []
    for c in range(nch):
        s = c % 2
        P.dma("sp", st[s][:], x[:, c * CAST_CH:(c + 1) * CAST_CH], writes=[ts[s]])
        if s == 0:
            P.op("dve", lambda E, s=s: E.tensor_copy(out=ob[s][:], in_=st[s][:]), reads=[ts[s]], writes=[to[s]])
        else:
            P.op("act", lambda E, s=s: E.activation(out=ob[s][:], in_=st[s][:], func=AF.Copy), reads=[ts[s]], writes=[to[s]])
        o = Tok(f"o{c}")
        P.dma("pool", y[:, c * CAST_CH:(c + 1) * CAST_CH], ob[s][:], reads=[to[s]], writes=[o], semname=f"ost{s}")
        outs.append(o)
    P.finish(outs)
    return nc


def emit_rmsnorm_to_bf16(P, src_ap, t_src, gain_sb, t_gain, sq, t_sq, st, t_st, dst_bf, t_dst, n=1024, extra_reads=()):
    P.op("act", lambda E: E.activation(out=sq, in_=src_ap, func=AF.Square), reads=[t_src] + list(extra_reads), writes=[t_sq])
    P.op("dve", lambda E: E.reduce_sum(out=st[:, 0:1], in_=sq, axis=AX.X), reads=[t_sq], writes=[t_st])
    P.op("act", lambda E: E.activation(out=st[:, 1:2], in_=st[:, 0:1], func=AF.Sqrt, bias=EPS, scale=1.0 / n), reads=[t_st], writes=[t_st])
    P.op("dve", lambda E: E.reciprocal(out=st[:, 2:3], in_=st[:, 1:2]), reads=[t_st], writes=[t_st])
    P.op("dve", lambda E: E.scalar_tensor_tensor(out=dst_bf, in0=src_ap, scalar=st[:, 2:3], in1=gain_sb, op0=ALU.mult, op1=ALU.mult),
         reads=[t_src, t_st, t_gain], writes=[t_dst])


def emit_A(nc, io, **opt):
    h = io("h", [TPC, D], F32)
    gpre = io("gpre", [128, D], F32)
    w = io("w", [128, 8, INW], BF16)
    lng = io("lng", [128, 512], F32)
    lnb = io("lnb", [128, 512], F32)
    wsT = io("wsT", [128, 8, 128], BF16)
    tri = io("tri", [128, 128], BF16)
    bsT = io("bsT", [128, 8], F32)
    ident_d = io("ident", [128, 128], BF16)
    q_o = io("q_o", [TPC, 512], BF16, out=True)
    kv_o = io("kv_o", [TPC, 768], BF16, out=True)
    g_o = io("g_o", [TPC, 24], F32, out=True)
    gm_o = io("gm_o", [TPC, 512], BF16, out=True)
    P = Prog(nc)
    wb = P.sb("wb", [128, 8, INW], BF16); t_wb = [Tok(f"wb{k}") for k in range(8)]
    for k in range(8):
        P.dma("sp" if k % 2 == 0 else "pool", wb[:, k, :], w[:, k, :], writes=[t_wb[k]])
    gp = P.sb("gp", [128, D], F32); t_gp = Tok("gp")
    P.dma("sp", gp[:], gpre[:, :], writes=[t_gp])
    lg = P.sb("lg", [128, 512], F32); t_lg = Tok("lg")
    lb = P.sb("lb", [128, 512], F32); t_lb = Tok("lb")
    P.dma("sp", lg[:], lng[:, :], writes=[t_lg])
    P.dma("sp", lb[:], lnb[:, :], writes=[t_lb])
    ws = P.sb("ws", [128, 8, 128], BF16); t_ws = Tok("ws")
    trs = P.sb("trs", [128, 128], BF16); t_tr = Tok("tr")
    P.dma("pool", ws[:], wsT[:, :, :], writes=[t_ws])
    P.dma("pool", trs[:], tri[:, :], writes=[t_tr])
    P.op("dve", lambda E: E.tensor_tensor(out=ws[:], in0=ws[:], in1=trs[:].unsqueeze(1).broadcast_to([128, 8, 128]), op=ALU.mult),
         reads=[t_tr, t_ws], writes=[t_ws])
    bs = P.sb("bs", [128, 8], F32); t_bs = Tok("bs")
    P.dma("pool", bs[:], bsT[:, :], writes=[t_bs])
    idn = P.sb("idn", [128, 128], BF16); t_id = Tok("idn")
    P.dma("pool", idn[:], ident_d[:, :], writes=[t_id])

    hf = [P.sb(f"hf{i}", [128, D], F32) for i in range(2)]; t_hf = [Tok(f"hf{i}") for i in range(2)]
    sq = P.sb("sq", [128, D], F32); t_sq = Tok("sq")
    st = P.sb("st", [128, 16], F32); t_st = Tok("st")
    hn = P.sb("hn", [128, D], BF16); t_hn = Tok("hn")
    hnT = P.sb("hnT", [128, 8, 128], BF16); t_hnT = Tok("hnT")
    TP = P.ps("TP", [128, 8, 128], BF16); t_TP = Tok("TP")
    PJ = [P.ps(f"PJ{i}", [128, 512]) for i in range(5)]; t_PJ = [Tok(f"PJ{i}") for i in range(5)]
    SV = P.ps("SV", [128, 512]); t_SV = Tok("SV")
    qo = [P.sb(f"qo{i}", [128, 512], BF16) for i in range(2)]; t_qo = [Tok(f"qo{i}") for i in range(2)]
    kvo = [P.sb(f"kvo{i}", [128, 768], BF16) for i in range(2)]; t_kvo = [Tok(f"kvo{i}") for i in range(2)]
    go = [P.sb(f"go{i}", [128, 24], F32) for i in range(2)]; t_go = [Tok(f"go{i}") for i in range(2)]
    gmo = [P.sb(f"gmo{i}", [128, 512], BF16) for i in range(2)]; t_gmo = [Tok(f"gmo{i}") for i in range(2)]
    zu = P.sb("zu", [128, 512], F32); t_zu = Tok("zu")
    gv = P.sb("gv", [128, 512], F32); t_gv = Tok("gv")
    zv = P.sb("zv", [128, 512], F32); t_zv = Tok("zv")
    zvb = P.sb("zvb", [128, 512], BF16); t_zvb = Tok("zvb")
    widths = [512, 512, 280, 512, 512]
    offs = [0, 512, 1024, 1304, 1816]
    outs = []
    for i in range(NT):
        s = i % 2
        P.dma("sp", hf[s][:], h[i * 128:(i + 1) * 128, :], writes=[t_hf[s]])
        emit_rmsnorm_to_bf16(P, hf[s][:], t_hf[s], gp[:], t_gp, sq[:], t_sq, st, t_st, hn[:], t_hn)
        for k in range(8):
            P.op("pe", lambda E, k=k: E.transpose(out=TP[:, k, :], in_=hn[:, k * 128:(k + 1) * 128], identity=idn[:]),
                 reads=[t_hn, t_id], writes=[t_TP])
        P.op("act", lambda E: E.activation(out=hnT[:], in_=TP[:], func=AF.Copy), reads=[t_TP], writes=[t_hnT])
        for c in range(5):
            for k in range(8):
                P.op("pe", lambda E, c=c, k=k: E.matmul(PJ[c][:, 0:widths[c]], lhsT=hnT[:, k, :], rhs=wb[:, k, offs[c]:offs[c] + widths[c]],
                                                         start=(k == 0), stop=(k == 7)),
                     reads=[t_hnT, t_wb[k]], writes=[t_PJ[c]])
        P.op("act", lambda E, s=s: E.activation(out=qo[s][:], in_=PJ[0][:], func=AF.Copy, scale=0.125), reads=[t_PJ[0]], writes=[t_qo[s]])
        P.op("dve", lambda E, s=s: E.tensor_copy(out=kvo[s][:, 0:512], in_=PJ[1][:]), reads=[t_PJ[1]], writes=[t_kvo[s]])
        P.op("dve", lambda E, s=s: E.tensor_copy(out=kvo[s][:, 512:768], in_=PJ[2][:, 0:256]), reads=[t_PJ[2], t_kvo[s]], writes=[t_kvo[s]])
        P.op("act", lambda E, s=s: E.activation(out=go[s][:], in_=PJ[2][:, 256:280], func=AF.Sigmoid), reads=[t_PJ[2]], writes=[t_go[s]])
        P.op("act", lambda E: E.activation(out=zu[:], in_=PJ[3][:], func=AF.Gelu_apprx_tanh), reads=[t_PJ[3]], writes=[t_zu])
        P.op("act", lambda E: E.activation(out=gv[:], in_=PJ[4][:], func=AF.Gelu_apprx_tanh), reads=[t_PJ[4]], writes=[t_gv])
        P.op("dve", lambda E: E.reduce_sum(out=st[:, 4:5], in_=gv[:], axis=AX.X), reads=[t_gv], writes=[t_st])
        P.op("act", lambda E: E.activation(out=sq[:, 0:512], in_=gv[:], func=AF.Square), reads=[t_gv], writes=[t_sq])
        P.op("dve", lambda E: E.reduce_sum(out=st[:, 5:6], in_=sq[:, 0:512], axis=AX.X), reads=[t_sq], writes=[t_st])
        P.op("dve", lambda E: E.tensor_scalar(out=st[:, 6:7], in0=st[:, 4:5], scalar1=1.0 / 512, scalar2=None, op0=ALU.mult), reads=[t_st], writes=[t_st])
        P.op("dve", lambda E: E.tensor_tensor(out=st[:, 7:8], in0=st[:, 6:7], in1=st[:, 6:7], op=ALU.mult), reads=[t_st], writes=[t_st])
        P.op("dve", lambda E: E.scalar_tensor_tensor(out=st[:, 8:9], in0=st[:, 5:6], scalar=1.0 / 512, in1=st[:, 7:8], op0=ALU.mult, op1=ALU.subtract),
             reads=[t_st], writes=[t_st])
        P.op("act", lambda E: E.activation(out=st[:, 9:10], in_=st[:, 8:9], func=AF.Sqrt, bias=1e-5, scale=1.0), reads=[t_st], writes=[t_st])
        P.op("dve", lambda E: E.reciprocal(out=st[:, 10:11], in_=st[:, 9:10]), reads=[t_st], writes=[t_st])
        P.op("dve", lambda E: E.tensor_scalar(out=zv[:], in0=gv[:], scalar1=st[:, 6:7], scalar2=st[:, 10:11], op0=ALU.subtract, op1=ALU.mult),
             reads=[t_gv, t_st], writes=[t_zv])
        P.op("dve", lambda E: E.tensor_tensor(out=zv[:], in0=zv[:], in1=lg[:], op=ALU.mult), reads=[t_zv, t_lg], writes=[t_zv])
        P.op("dve", lambda E: E.tensor_tensor(out=zvb[:], in0=zv[:], in1=lb[:], op=ALU.add), reads=[t_zv, t_lb], writes=[t_zvb])
        for g in range(8):
            P.op("pe", lambda E, g=g: E.matmul(SV[:, g * 64:(g + 1) * 64], lhsT=ws[:, g, :], rhs=zvb[:, g * 64:(g + 1) * 64],
                                               start=(g == 0), stop=(g == 7), skip_group_check=True),
                 reads=[t_ws, t_zvb], writes=[t_SV])
        for g in range(8):
            P.op("dve", lambda E, g=g, s=s: E.scalar_tensor_tensor(out=gmo[s][:, g * 64:(g + 1) * 64], in0=SV[:, g * 64:(g + 1) * 64],
                                                                  scalar=bs[:, g:g + 1], in1=zu[:, g * 64:(g + 1) * 64], op0=ALU.add, op1=ALU.mult),
                 reads=[t_SV, t_bs, t_zu] + ([t_gmo[s]] if g else []), writes=[t_gmo[s]])
        for nm, dst, src, tk in (("q", q_o, qo, t_qo), ("kv", kv_o, kvo, t_kvo), ("g", g_o, go, t_go), ("gm", gm_o, gmo, t_gmo)):
            o = Tok(f"o_{nm}{i}")
            P.dma("pool", dst[i * 128:(i + 1) * 128, :], src[s][:], reads=[tk[s]], writes=[o], semname=f"o_{nm}{s}")
            outs.append(o)
    P.finish(outs)
    nc.all_engine_barrier()


def build_A():
    nc = _new_nc()
    emit_A(nc, IO(nc))
    return nc


_PROGS = {}


def _prog(name, builder, *a):
    key = (name,) + tuple(a)
    if key not in _PROGS:
        _PROGS[key] = builder(*a)
    return _PROGS[key]


def _run(nc, in_maps):
    import time as _t
    t0 = _t.time()
    res = run_bass_kernel_spmd(nc, in_maps, core_ids=list(range(NCORES)))
    print("[launch] %.1fs" % (_t.time() - t0), flush=True)
    return res.results


def _raw(a):
    a = np.asarray(a)
    return a.view(np.uint16) if a.dtype == NPBF else a


def _unraw(a, bf):
    return a.view(NPBF) if bf else a


def _const(v, bf):
    return np.array(v, np.float32).astype(NPBF).view(np.uint16) if bf else np.float64(v)


def _tr(a, axes):
    if a.dtype == NPBF:
        return np.ascontiguousarray(a.view(np.uint16).transpose(axes)).view(NPBF)
    return np.ascontiguousarray(a.transpose(axes))


def _rep(v, n=128):
    return np.ascontiguousarray(np.broadcast_to(np.asarray(v, np.float32)[None, :], (n, v.shape[0])))


def cast_all(arrs):
    flat = np.concatenate([np.asarray(a, np.float32).ravel() for a in arrs])
    per = NCORES * 128 * CAST_CH
    nch = -(-flat.size // per)
    pad = np.zeros(nch * per, np.float32)
    pad[:flat.size] = flat
    x = pad.reshape(NCORES, 128, nch * CAST_CH)
    nc = _prog("cast", build_cast, nch)
    res = _run(nc, [{"x": x[r]} for r in range(NCORES)])
    y = np.concatenate([np.asarray(res[r]["y"]).reshape(-1) for r in range(NCORES)])
    out = []
    o = 0
    for a in arrs:
        out.append(y[o:o + a.size].reshape(a.shape))
        o += a.size
    return out


IN_PERM = np.concatenate([np.arange(0, 1280), np.arange(1280, 1304), np.arange(1304, 2328)])


def _consts():
    c = {}
    c["ident"] = np.eye(128, dtype=np.float32).astype(NPBF)
    s = np.arange(128)
    c["tri"] = (s[:, None] <= s[None, :]).astype(np.float32).astype(NPBF)
    return c


def run_A(h, l, W, C):
    nc = _prog("A", build_A)
    wb = np.ascontiguousarray(W["w_in"][l].reshape(8, 128, INW).transpose(1, 0, 2))
    wsT = np.ascontiguousarray(W["gmlp_w_s"][l].transpose(2, 0, 1))
    common = {"gpre": _rep(W["norm_mix_pre"][l]), "w": wb, "lng": _rep(W["gmlp_ln_g"][l]), "lnb": _rep(W["gmlp_ln_b"][l]),
              "wsT": wsT, "tri": C["tri"], "bsT": np.ascontiguousarray(W["gmlp_b_s"][l].T), "ident": C["ident"]}
    maps = [dict(common, h=h[r * TPC:(r + 1) * TPC]) for r in range(NCORES)]
    res = _run(nc, maps)
    cat = lambda k: np.concatenate([np.asarray(res[r][k]) for r in range(NCORES)], 0)
    return cat("q_o"), cat("kv_o"), cat("g_o"), cat("gm_o")


def emit_C1(nc, io, **opt):
    h = io("h", [TPC, D], F32)
    mixT = io("mixT", [128, NT, 8, 128], BF16)
    w = io("w", [128, 8, D], BF16)
    gpost = io("gpost", [128, D], F32)
    h_o = io("h_o", [TPC, D], F32, out=True)
    P = Prog(nc)
    wb = P.sb("wb", [128, 8, D], BF16); t_wb = Tok("wb")
    P.dma("pool", wb[:], w[:, :, :], writes=[t_wb])
    gp = P.sb("gp", [128, D], F32); t_gp = Tok("gp")
    P.dma("pool", gp[:], gpost[:, :], writes=[t_gp])
    hf = [P.sb(f"hf{i}", [128, D], F32) for i in range(2)]; t_hf = [Tok(f"hf{i}") for i in range(2)]
    mx = [P.sb(f"mx{i}", [128, 8, 128], BF16) for i in range(2)]; t_mx = [Tok(f"mx{i}") for i in range(2)]
    sq = P.sb("sq", [128, D], F32); t_sq = Tok("sq")
    st = P.sb("st", [128, 4], F32); t_st = Tok("st")
    tn = P.sb("tn", [128, D], F32); t_tn = Tok("tn")
    ho = [P.sb(f"ho{i}", [128, D], F32) for i in range(2)]; t_ho = [Tok(f"ho{i}") for i in range(2)]
    MX = [P.ps(f"MX{i}", [128, 2, 512]) for i in range(2)]; t_MX = [Tok(f"MX{i}") for i in range(2)]
    outs = []
    for i in range(NT):
        s = i % 2
        P.dma("sp", hf[s][:], h[i * 128:(i + 1) * 128, :], writes=[t_hf[s]])
        P.dma("sp", mx[s][:], mixT[:, i, :, :], writes=[t_mx[s]])
        for hh in range(2):
            for k in range(8):
                P.op("pe", lambda E, hh=hh, k=k, s=s: E.matmul(MX[s][:, hh, :], lhsT=mx[s][:, k, :], rhs=wb[:, k, hh * 512:(hh + 1) * 512],
                                                              start=(k == 0), stop=(k == 7)),
                     reads=[t_mx[s], t_wb], writes=[t_MX[s]])
        mxv = MX[s][:].rearrange("p a b -> p (a b)")
        emit_rmsnorm_to_bf16(P, mxv, t_MX[s], gp[:], t_gp, sq[:], t_sq, st, t_st, tn[:], t_tn)
        P.op("dve", lambda E, s=s: E.tensor_tensor(out=ho[s][:], in0=hf[s][:], in1=tn[:], op=ALU.add), reads=[t_hf[s], t_tn], writes=[t_ho[s]])
        o = Tok(f"o{i}")
        P.dma("pool", h_o[i * 128:(i + 1) * 128, :], ho[s][:], reads=[t_ho[s]], writes=[o], semname=f"o{s}")
        outs.append(o)
    P.finish(outs)
    nc.all_engine_barrier()


def build_C1():
    nc = _new_nc()
    emit_C1(nc, IO(nc))
    return nc


NF = DFF // 128


def emit_C2(nc, io, **opt):
    h = io("h", [TPC, D], F32)
    wgu = io("wgu", [128, 8, 2 * DFF], BF16)
    wdn = io("wdn", [128, NF, D], BF16)
    gpre = io("gpre", [128, D], F32)
    gpost = io("gpost", [128, D], F32)
    ident_d = io("ident", [128, 128], BF16)
    h_o = io("h_o", [TPC, D], F32, out=True)
    P = Prog(nc)
    wg = P.sb("wg", [128, 8, 2 * DFF], BF16); t_wg = [Tok(f"wg{k}") for k in range(8)]
    for k in range(8):
        P.dma("sp" if k % 2 == 0 else "pool", wg[:, k, :], wgu[:, k, :], writes=[t_wg[k]])
    wd = P.sb("wd", [128, NF, D], BF16); t_wd = Tok("wd")
    P.dma("pool", wd[:], wdn[:, :, :], writes=[t_wd])
    g1 = P.sb("g1", [128, D], F32); t_g1 = Tok("g1")
    g2 = P.sb("g2", [128, D], F32); t_g2 = Tok("g2")
    P.dma("sp", g1[:], gpre[:, :], writes=[t_g1])
    P.dma("sp", g2[:], gpost[:, :], writes=[t_g2])
    idn = P.sb("idn", [128, 128], BF16); t_id = Tok("idn")
    P.dma("sp", idn[:], ident_d[:, :], writes=[t_id])
    hf = [P.sb(f"hf{i}", [128, D], F32) for i in range(4)]; t_hf = [Tok(f"hf{i}") for i in range(4)]
    sq = P.sb("sq", [128, D], F32); t_sq = Tok("sq")
    st = P.sb("st", [128, 4], F32); t_st = Tok("st")
    hn = P.sb("hn", [128, D], BF16); t_hn = Tok("hn")
    hnT = P.sb("hnT", [128, 8, 512], BF16); t_hnT = Tok("hnT")
    sg = [P.sb(f"sg{i}", [128, 512], F32) for i in range(2)]; t_sg = [Tok(f"sg{i}") for i in range(2)]
    actT = P.sb("actT", [128, NF, 512], BF16); t_act = [Tok(f"act{f}") for f in range(NF)]
    ho = [P.sb(f"ho{i}", [128, D], F32) for i in range(2)]; t_ho = [Tok(f"ho{i}") for i in range(2)]
    TP = P.ps("TP", [128, 8, 128], BF16); t_TP = Tok("TP")
    GP = [P.ps(f"GP{i}", [128, 512]) for i in range(2)]; t_GP = [Tok(f"GP{i}") for i in range(2)]
    UP = [P.ps(f"UP{i}", [128, 512]) for i in range(2)]; t_UP = [Tok(f"UP{i}") for i in range(2)]
    DN = P.ps("DN", [128, 2, 512]); t_DN = Tok("DN")
    outs = []
    for gi in range(NT // 4):
        for j in range(4):
            i = gi * 4 + j
            P.dma("sp", hf[j][:], h[i * 128:(i + 1) * 128, :], writes=[t_hf[j]])
            emit_rmsnorm_to_bf16(P, hf[j][:], t_hf[j], g1[:], t_g1, sq[:], t_sq, st, t_st, hn[:], t_hn)
            for k in range(8):
                P.op("pe", lambda E, k=k: E.transpose(out=TP[:, k, :], in_=hn[:, k * 128:(k + 1) * 128], identity=idn[:]),
                     reads=[t_hn, t_id], writes=[t_TP])
            P.op("act", lambda E, j=j: E.activation(out=hnT[:, :, j * 128:(j + 1) * 128], in_=TP[:], func=AF.Copy), reads=[t_TP], writes=[t_hnT])
        for f in range(NF):
            s = f % 2
            for k in range(8):
                P.op("pe", lambda E, f=f, k=k, s=s: E.matmul(GP[s][:], lhsT=wg[:, k, f * 128:(f + 1) * 128], rhs=hnT[:, k, :], start=(k == 0), stop=(k == 7)),
                     reads=[t_wg[k], t_hnT], writes=[t_GP[s]])
            for k in range(8):
                P.op("pe", lambda E, f=f, k=k, s=s: E.matmul(UP[s][:], lhsT=wg[:, k, DFF + f * 128:DFF + (f + 1) * 128], rhs=hnT[:, k, :], start=(k == 0), stop=(k == 7)),
                     reads=[t_wg[k], t_hnT], writes=[t_UP[s]])
            P.op("act", lambda E, s=s: E.activation(out=sg[s][:], in_=GP[s][:], func=AF.Silu), reads=[t_GP[s]], writes=[t_sg[s]])
            P.op("dve", lambda E, s=s, f=f: E.tensor_tensor(out=actT[:, f, :], in0=sg[s][:], in1=UP[s][:], op=ALU.mult), reads=[t_sg[s], t_UP[s]], writes=[t_act[f]])
        for j in range(4):
            i = gi * 4 + j
            s = i % 2
            for hh in range(2):
                for f in range(NF):
                    P.op("pe", lambda E, hh=hh, f=f, j=j: E.matmul(DN[:, hh, :], lhsT=actT[:, f, j * 128:(j + 1) * 128], rhs=wd[:, f, hh * 512:(hh + 1) * 512],
                                                                   start=(f == 0), stop=(f == NF - 1)),
                         reads=[t_act[f], t_wd], writes=[t_DN])
            dnv = DN[:].rearrange("p a b -> p (a b)")
            emit_rmsnorm_to_bf16(P, dnv, t_DN, g2[:], t_g2, sq[:], t_sq, st, t_st, ho[s][:], t_ho[s])
            P.op("dve", lambda E, s=s, j=j: E.tensor_tensor(out=ho[s][:], in0=hf[j][:], in1=ho[s][:], op=ALU.add), reads=[t_hf[j], t_ho[s]], writes=[t_ho[s]])
            o = Tok(f"o{i}")
            P.dma("pool", h_o[i * 128:(i + 1) * 128, :], ho[s][:], reads=[t_ho[s]], writes=[o], semname=f"o{s}")
            outs.append(o)
    P.finish(outs)
    nc.all_engine_barrier()


def build_C2():
    nc = _new_nc()
    emit_C2(nc, IO(nc))
    return nc


def run_C1(h, mixed, l, W):
    nc = _prog("C1", build_C1)
    wb = np.ascontiguousarray(W["w_out"][l].reshape(8, 128, D).transpose(1, 0, 2))
    gp = _rep(W["norm_mix_post"][l])
    maps = []
    for r in range(NCORES):
        m = mixed[r * TPC:(r + 1) * TPC].reshape(NT, 128, 8, 128)
        maps.append({"h": h[r * TPC:(r + 1) * TPC], "mixT": _tr(m, (3, 0, 2, 1)), "w": wb, "gpost": gp})
    res = _run(nc, maps)
    return np.concatenate([np.asarray(res[r]["h_o"]) for r in range(NCORES)], 0)


def run_C2(h, l, W, C):
    nc = _prog("C2", build_C2)
    wgu = np.ascontiguousarray(W["w_gate_up"][l].reshape(8, 128, 2 * DFF).transpose(1, 0, 2))
    wdn = np.ascontiguousarray(W["w_down"][l].reshape(NF, 128, D).transpose(1, 0, 2))
    common = {"wgu": wgu, "wdn": wdn, "gpre": _rep(W["norm_ffn_pre"][l]), "gpost": _rep(W["norm_ffn_post"][l]), "ident": C["ident"]}
    res = _run(nc, [dict(common, h=h[r * TPC:(r + 1) * TPC]) for r in range(NCORES)])
    return np.concatenate([np.asarray(res[r]["h_o"]) for r in range(NCORES)], 0)


def emit_Bc(nc, io, **opt):
    xT = io("xT", [64, S + 16], BF16)
    w1 = io("w1", [64, 32, 256], BF16)
    posT = io("posT", [64, 32], BF16)
    w2 = io("w2", [128, 2, 64], BF16)
    ccT_o = io("ccT", [64, 1024], BF16, out=True)
    cc_o = io("cc", [1024, 64], BF16, out=True)
    P = Prog(nc)
    xs = P.sb("xs", [64, S + 16], BF16); t_xs = Tok("xs")
    P.dma("sp", xs[:], xT[:, :], writes=[t_xs])
    w1s = P.sb("w1s", [64, 32, 256], BF16); t_w1 = Tok("w1")
    P.dma("pool", w1s[:], w1[:, :, :], writes=[t_w1])
    ps_ = P.sb("ps_", [64, 32], BF16); t_ps = Tok("ps")
    P.dma("pool", ps_[:], posT[:, :], writes=[t_ps])
    w2s = P.sb("w2s", [128, 2, 64], BF16); t_w2 = Tok("w2")
    P.dma("pool", w2s[:], w2[:, :, :], writes=[t_w2])
    PB = P.ps("PB", [128, 2]); t_PB = Tok("PB")
    H = [P.ps(f"H{c}", [128, 512]) for c in range(2)]; t_H = [Tok(f"H{c}") for c in range(2)]
    CT = P.ps("CT", [64, 512]); t_CT = Tok("CT")
    CC = P.ps("CC", [128, 4, 64]); t_CC = Tok("CC")
    posb = P.sb("posb", [128, 2], F32); t_posb = Tok("posb")
    GT = P.sb("GT", [128, 2, 512], BF16); t_GT = Tok("GT")
    cts = P.sb("cts", [64, 512], BF16); t_cts = Tok("cts")
    ccs = P.sb("ccs", [128, 4, 64], BF16); t_ccs = Tok("ccs")
    for c in range(2):
        for l in range(32):
            P.op("pe", lambda E, c=c, l=l: E.matmul(PB[:, c:c + 1], lhsT=w1s[:, l, c * 128:(c + 1) * 128], rhs=ps_[:, l:l + 1], start=(l == 0), stop=(l == 31)),
                 reads=[t_w1, t_ps], writes=[t_PB])
    P.op("dve", lambda E: E.tensor_copy(out=posb[:], in_=PB[:]), reads=[t_PB], writes=[t_posb])
    outs = []
    cc_v = cc_o.rearrange("(t p) d -> p t d", p=128)
    for j in range(2):
        base = j * 8192
        for c in range(2):
            for l in range(32):
                b0 = base + (16 if l >= 16 else 0)
                rhs = xs[:, b0:b0 + 8192].rearrange("p (i s) -> p i s", s=16)[:, :, l % 16]
                P.op("pe", lambda E, c=c, l=l, rhs=rhs: E.matmul(H[c][:], lhsT=w1s[:, l, c * 128:(c + 1) * 128], rhs=rhs, start=(l == 0), stop=(l == 31)),
                     reads=[t_w1, t_xs], writes=[t_H[c]])
            P.op("act", lambda E, c=c: E.activation(out=GT[:, c, :], in_=H[c][:], func=AF.Gelu_apprx_tanh, bias=posb[:, c:c + 1]),
                 reads=[t_H[c], t_posb], writes=[t_GT])
        for c in range(2):
            P.op("pe", lambda E, c=c: E.matmul(CT[:], lhsT=w2s[:, c, :], rhs=GT[:, c, :], start=(c == 0), stop=(c == 1)), reads=[t_w2, t_GT], writes=[t_CT])
        P.op("dve", lambda E: E.tensor_copy(out=cts[:], in_=CT[:]), reads=[t_CT], writes=[t_cts])
        o = Tok(f"oT{j}")
        P.dma("sp", ccT_o[:, j * 512:(j + 1) * 512], cts[:], reads=[t_cts], writes=[o], semname="oT")
        outs.append(o)
        for bt in range(4):
            for c in range(2):
                P.op("pe", lambda E, c=c, bt=bt: E.matmul(CC[:, bt, :], lhsT=GT[:, c, bt * 128:(bt + 1) * 128], rhs=w2s[:, c, :], start=(c == 0), stop=(c == 1)),
                     reads=[t_w2, t_GT], writes=[t_CC])
        P.op("dve", lambda E: E.tensor_copy(out=ccs[:], in_=CC[:]), reads=[t_CC], writes=[t_ccs])
        o = Tok(f"oc{j}")
        P.dma("sp", cc_v[:, j * 4:(j + 1) * 4, :], ccs[:], reads=[t_ccs], writes=[o], semname="oc")
        outs.append(o)
    P.finish(outs)
    nc.all_engine_barrier()


def build_Bc():
    nc = _new_nc()
    emit_Bc(nc, IO(nc))
    return nc


NPAIR = 64
NADD = 25


def emit_B(nc, io, **opt):
    qd = io("qd", [65, NPAIR, 1024], BF16)
    kTs_d = io("kTs", [65, S], BF16)
    vs_d = io("vs", [128, 128, 65], BF16)
    kTw_d = io("kTw", [65, S], BF16)
    vw_d = io("vw", [128, 128, 65], BF16)
    cmp_src = opt.get("cmp_src")
    if cmp_src is None:
        kcT_d = io("kcT", [65, 1024], BF16)
        vc_d = io("vc", [128, 8, 65], BF16)
    else:
        onesrow_d = io("onesrow", [1, 1024], BF16)
        vc1_d = io("vc1", [128, 8, 1], BF16)
        zrow_d = io("zrow", [1, 65], BF16)
    gts_d = io("gts", [128, 128, 6], F32)
    ov_d = io("ov", [128, 8, 256], BF16)
    ex_d = io("expall", [128, 64, 128], BF16)
    id_d = io("ident", [128, 128], BF16)
    tv_d = io("tv", [128, 512], F32)
    tf_d = io("tf", [128, 512], F32)
    tva_d = io("tvalid", [128, 512], F32)
    bg_d = io("bg", [128, NADD, 512], BF16)
    fb_d = io("farb", [128, 2, 4], BF16)
    o_d = io("o", [S, 128], BF16, out=True)
    P = Prog(nc)

    def load(name, d, shape, dt, eng):
        t = P.sb(name, shape, dt)
        tk = Tok(name)
        P.dma(eng, t[:], d, writes=[tk])
        return t, tk
    if cmp_src is None:
        kcT, t_kcT = load("kcT_s", kcT_d[:, :], [65, 1024], BF16, "sp")
        vc, t_vc = load("vc_s", vc_d[:, :, :], [128, 8, 65], BF16, "sp")
    else:
        kcT = P.sb("kcT_s", [65, 1024], BF16); t_kcT = Tok("kcT_s")
        vc = P.sb("vc_s", [128, 8, 65], BF16); t_vc = Tok("vc_s")
        P.dma("sp", kcT[0:64, :], cmp_src[0][:, :], writes=[t_kcT])
        P.dma("sp", kcT[64:65, :], onesrow_d[:, :], writes=[t_kcT])
        P.dma("sp", vc[:, :, 0:64], cmp_src[1].rearrange("(t p) d -> p t d", p=128), writes=[t_vc])
        P.dma("sp", vc[:, :, 64:65], vc1_d[:, :, :], writes=[t_vc], allow_slow_non_contiguous=True)
        P.dma("sp", vc[127:128, 7, :], zrow_d[:, :], writes=[t_vc])
    ov, t_ov = load("ov_s", ov_d[:, :, :], [128, 8, 256], BF16, "sp")
    idn, t_id = load("id_s", id_d[:, :], [128, 128], BF16, "sp")
    add, t_add = load("add_s", bg_d[:, :, :], [128, NADD, 512], BF16, "sp")
    fb, t_fb = load("fb_s", fb_d[:, :, :], [128, 2, 4], BF16, "sp")
    gts, t_gts = load("gts_s", gts_d[:, :, :], [128, 128, 6], F32, "sp")
    tv, t_tv = load("tv_s", tv_d[:, :], [128, 512], F32, "sp")
    tf, t_tf = load("tf_s", tf_d[:, :], [128, 512], F32, "sp")
    tva, t_tva = load("tva_s", tva_d[:, :], [128, 512], F32, "sp")
    kTw, t_kTw = load("kTw_s", kTw_d[:, :], [65, S], BF16, "pool")
    vw, t_vw = load("vw_s", vw_d[:, :, :], [128, 128, 65], BF16, "pool")
    kTs, t_kTs = load("kTs_s", kTs_d[:, :], [65, S], BF16, "sp")
    vs, t_vs = load("vs_s", vs_d[:, :, :], [128, 128, 65], BF16, "pool")
    ex, t_ex = load("ex_s", ex_d[:, :, :], [128, 64, 128], BF16, "pool")
    for i in range(NADD):
        which = 0 if i < 17 else 1
        a3 = add[:, i, :].rearrange("p (x q) -> p x q", q=128)
        P.op("dve", lambda E, a3=a3, which=which: E.tensor_tensor(out=a3, in0=a3, in1=fb[:, which, :].unsqueeze(2).broadcast_to([128, 4, 128]), op=ALU.subtract),
             reads=[t_fb, t_add], writes=[t_add])

    qs = [P.sb(f"qs{i}", [65, 2, 2, 2, 128], BF16) for i in range(2)]; t_qs = [Tok(f"qs{i}") for i in range(2)]
    es = [P.sb(f"es{i}", [128, 512], BF16) for i in range(4)]; t_es = [Tok(f"es{i}") for i in range(4)]
    Lb = [P.ps(f"L{i}", [128, 512]) for i in range(3)]; t_L = [Tok(f"L{i}") for i in range(3)]
    OC = P.ps("OC", [128, 512]); t_OC = Tok("OC")
    IMP = [P.ps(f"IMP{i}", [128, 512]) for i in range(2)]; t_IMP = Tok("IMP")
    OS = P.ps("OS", [128, 512]); t_OS = Tok("OS")
    OW = P.ps("OW", [128, 512]); t_OW = Tok("OW")
    TPn = IMP[1][:].bitcast(BF16)[:, 0:512].rearrange("p (t c q) -> p t c q", t=2, c=2); t_TPn = t_IMP
    nsT = P.sb("nsT", [128, 2, 2, 128], BF16); t_nsT = Tok("nsT")
    negsel = [P.sb(f"negsel{t}", [128, 256], BF16) for t in range(2)]; t_neg = [Tok(f"neg{t}") for t in range(2)]
    acc = [P.sb(f"acc{t}", [128, 2, 64], F32) for t in range(2)]; t_acc = [Tok(f"acc{t}") for t in range(2)]
    ob = [P.sb(f"ob{t}", [128, 128], BF16) for t in range(2)]; t_ob = [Tok(f"ob{t}") for t in range(2)]
    rz = P.sb("rz", [128, 8], F32); t_rz = Tok("rz")
    rz2 = P.sb("rz2", [128, 8], F32); t_rz2 = Tok("rz2")
    imp = P.sb("imp", [128, 256], F32); t_imp = Tok("imp")
    score = P.sb("score", [128, 256], F32); t_score = Tok("score")
    scr = P.sb("scr", [128, 256], F32); t_scr = Tok("scr")
    sel = P.sb("sel", [128, 256], F32); t_sel = Tok("sel")
    m8 = P.sb("m8", [128, 16], F32); t_m8 = Tok("m8")
    OCv = OC[:, 0:260].rearrange("p (r c) -> p r c", c=65)
    OSv = OS[:, 0:260].rearrange("p (r c) -> p r c", c=65)
    OWv = OW[:, 0:260].rearrange("p (r c) -> p r c", c=65)
    outs = []
    state = {"pending": [], "li": 0, "ei": 0}
    DEPTH = 2

    def pop():
        p = state["pending"].pop(0)
        p["exp"]()
        p["pv"]()
        if p.get("after"):
            p["after"]()

    def flush():
        while state["pending"]:
            pop()

    def push(qk, pv, after=None):
        li = state["li"]; state["li"] = (li + 1) % 3
        ei = state["ei"]; state["ei"] = (ei + 1) % 4
        qk(li)

        def ex_(li=li, ei=ei):
            P.op("act", lambda E: E.activation(out=es[ei][:], in_=Lb[li][:], func=AF.Exp), reads=[t_L[li]], writes=[t_es[ei]])
        state["pending"].append({"exp": ex_, "pv": (lambda ei=ei: pv(ei)), "after": after})
        while len(state["pending"]) > DEPTH:
            pop()

    def mm(out, lhsT, rhs, start, stop, reads, writes):
        P.op("pe", lambda E: E.matmul(out, lhsT=lhsT, rhs=rhs, start=start, stop=stop, skip_group_check=True), reads=reads, writes=writes)

    for n in range(NPAIR):
        s = n % 2
        P.dma("sp", qs[s][:].rearrange("p a b c d -> p (a b c d)"), qd[:, n, :], writes=[t_qs[s]])
        for t in range(2):
            gt = 2 * n + t
            nkc = gt // 16 + 1
            for kc in range(nkc):
                d = gt - 16 * kc

                def qk(li, kc=kc, d=d, t=t, s=s):
                    mm(Lb[li][:, 0:256], kcT[0:65, kc * 128:(kc + 1) * 128], qs[s][0:65, 0, t, :, :], True, False, [t_kcT, t_qs[s]], [t_L[li]])
                    mm(Lb[li][:, 256:512], kcT[0:65, kc * 128:(kc + 1) * 128], qs[s][0:65, 1, t, :, :], False, d > 16, [t_kcT, t_qs[s]], [t_L[li]])
                    if d <= 16:
                        mm(Lb[li][:], idn[:], add[:, d, :], False, True, [t_id, t_add], [t_L[li]])

                def pv(ei, kc=kc, nkc=nkc):
                    for r in range(4):
                        mm(OC[:, r * 65:(r + 1) * 65], es[ei][:, r * 128:(r + 1) * 128], vc[:, kc, :], kc == 0 and r == 0, kc == nkc - 1, [t_es[ei], t_vc], [t_OC])
                    for r in range(4):
                        mm(IMP[r // 2][:, (r % 2) * 256:(r % 2) * 256 + 256], es[ei][:, r * 128:(r + 1) * 128], ov[:, kc, :], kc == 0 and r % 2 == 0, kc == nkc - 1,
                           [t_es[ei], t_ov], [t_IMP])

                def after(t=t, gt=gt):
                    off = 256 - 2 * gt
                    g3 = gts[:, gt, :].rearrange("p (h b) -> p h b", b=3)
                    P.op("dve", lambda E: E.tensor_scalar(out=rz[:, 0:4], in0=OCv[:, :, 64], scalar1=1e-30, scalar2=None, op0=ALU.max), reads=[t_OC], writes=[t_rz])
                    P.op("dve", lambda E: E.reciprocal(out=rz[:, 0:4], in_=rz[:, 0:4]), reads=[t_rz], writes=[t_rz])
                    P.op("dve", lambda E: E.tensor_tensor(out=rz[:, 4:6], in0=rz[:, 0:2], in1=g3[:, :, 0], op=ALU.mult), reads=[t_rz, t_gts], writes=[t_rz])
                    for hh in range(2):
                        P.op("dve", lambda E, hh=hh: E.tensor_scalar(out=acc[t][:, hh, :], in0=OCv[:, hh, 0:64], scalar1=rz[:, 4 + hh:5 + hh], scalar2=None, op0=ALU.mult),
                             reads=[t_OC, t_rz], writes=[t_acc[t]])
                    P.op("dve", lambda E: E.tensor_scalar(out=imp[:], in0=IMP[0][:, 0:256], scalar1=rz[:, 0:1], scalar2=None, op0=ALU.mult), reads=[t_IMP, t_rz], writes=[t_imp])
                    for r in range(1, 4):
                        P.op("dve", lambda E, r=r: E.scalar_tensor_tensor(out=imp[:], in0=IMP[r // 2][:, (r % 2) * 256:(r % 2) * 256 + 256], scalar=rz[:, r:r + 1], in1=imp[:],
                                                                         op0=ALU.mult, op1=ALU.add), reads=[t_IMP, t_rz, t_imp], writes=[t_imp])
                    P.op("dve", lambda E: E.tensor_tensor(out=score[:], in0=imp[:], in1=tv[:, off:off + 256], op=ALU.mult), reads=[t_imp, t_tv], writes=[t_score])
                    P.op("dve", lambda E: E.tensor_tensor(out=score[:], in0=score[:], in1=tf[:, off:off + 256], op=ALU.add), reads=[t_score, t_tf], writes=[t_score])
                    P.op("dve", lambda E: E.memset(score[:, 0:1], 3.0e9), reads=[t_score], writes=[t_score])
                    P.op("dve", lambda E: E.max(out=m8[:, 0:8], in_=score[:]), reads=[t_score], writes=[t_m8])
                    P.op("dve", lambda E: E.match_replace(out=scr[:], in_to_replace=m8[:, 0:8], in_values=score[:], imm_value=-3.0e9), reads=[t_score, t_m8], writes=[t_scr])
                    P.op("dve", lambda E: E.max(out=m8[:, 8:16], in_=scr[:]), reads=[t_scr, t_m8], writes=[t_m8])
                    P.op("dve", lambda E: E.tensor_scalar(out=sel[:], in0=score[:], scalar1=m8[:, 15:16], scalar2=None, op0=ALU.is_ge), reads=[t_score, t_m8], writes=[t_sel])
                    P.op("dve", lambda E: E.tensor_tensor(out=sel[:], in0=sel[:], in1=tva[:, off:off + 256], op=ALU.mult), reads=[t_sel, t_tva], writes=[t_sel])
                    P.op("dve", lambda E: E.tensor_scalar(out=negsel[t][:], in0=sel[:], scalar1=1.0, scalar2=-NEG, op0=ALU.subtract, op1=ALU.mult), reads=[t_sel], writes=[t_neg[t]])

                push(qk, pv, after if kc == nkc - 1 else None)
        wlist = [jp for jp in range(6) if 2 * n + jp - 4 >= 0]
        widx = {0: 20, 1: 21, 3: 22, 4: 23, 5: 24}
        for jp in wlist:
            kt = 2 * n + jp - 4

            def qk(li, kt=kt, jp=jp, s=s):
                mm(Lb[li][:], kTw[0:65, kt * 128:(kt + 1) * 128], qs[s][0:65, 0, :, :, :], True, jp == 2, [t_kTw, t_qs[s]], [t_L[li]])
                if jp != 2:
                    mm(Lb[li][:], idn[:], add[:, widx[jp], :], False, True, [t_id, t_add], [t_L[li]])

            def pv(ei, kt=kt, jp=jp, first=(jp == wlist[0]), last=(jp == wlist[-1])):
                for x in range(4):
                    mm(OW[:, x * 65:(x + 1) * 65], es[ei][:, x * 128:(x + 1) * 128], vw[:, kt, :], first and x == 0, last, [t_es[ei], t_vw], [t_OW])
            push(qk, pv)
        nks = 2 * n + 2
        for kt in range(nks):
            def qk(li, kt=kt, n=n, s=s):
                if kt == 0:
                    for t in range(2):
                        for c in range(2):
                            P.op("pe", lambda E, t=t, c=c: E.transpose(out=TPn[:, t, c, :], in_=negsel[t][:, c * 128:(c + 1) * 128], identity=idn[:]),
                                 reads=[t_neg[t], t_id], writes=[t_TPn])
                    P.op("act", lambda E: E.activation(out=nsT[:], in_=TPn, func=AF.Copy), reads=[t_TPn], writes=[t_nsT])
                mm(Lb[li][:], kTs[0:65, kt * 128:(kt + 1) * 128], qs[s][0:65, 0, :, :, :], True, False, [t_kTs, t_qs[s]], [t_L[li]])
                mm(Lb[li][:].rearrange("p (t h q) -> p t h q", t=2, h=2), ex[:, kt % 64, :],
                   nsT[:, :, kt // 64, :].unsqueeze(2).broadcast_to([128, 2, 2, 128]), False, kt < 2 * n - 1, [t_ex, t_nsT], [t_L[li]])
                if kt >= 2 * n - 1:
                    mm(Lb[li][:], idn[:], add[:, 17 + kt - (2 * n - 1), :], False, True, [t_id, t_add], [t_L[li]])

            def pv(ei, kt=kt, nks=nks):
                for x in range(4):
                    mm(OS[:, x * 65:(x + 1) * 65], es[ei][:, x * 128:(x + 1) * 128], vs[:, kt, :], kt == 0 and x == 0, kt == nks - 1, [t_es[ei], t_vs], [t_OS])

            def epi(t, gt):
                g3 = gts[:, gt, :].rearrange("p (h b) -> p h b", b=3)
                for (Ov, tO, o0, br) in ((OSv, t_OS, 0, 1), (OWv, t_OW, 4, 2)):
                    P.op("dve", lambda E, Ov=Ov, o0=o0: E.tensor_scalar(out=rz2[:, o0:o0 + 2], in0=Ov[:, 2 * t:2 * t + 2, 64], scalar1=1e-30, scalar2=None, op0=ALU.max),
                         reads=[tO], writes=[t_rz2])
                    P.op("dve", lambda E, o0=o0: E.reciprocal(out=rz2[:, o0:o0 + 2], in_=rz2[:, o0:o0 + 2]), reads=[t_rz2], writes=[t_rz2])
                    P.op("dve", lambda E, o0=o0, br=br: E.tensor_tensor(out=rz2[:, o0 + 2:o0 + 4], in0=rz2[:, o0:o0 + 2], in1=g3[:, :, br], op=ALU.mult),
                         reads=[t_rz2, t_gts], writes=[t_rz2])
                for hh in range(2):
                    P.op("dve", lambda E, hh=hh: E.scalar_tensor_tensor(out=acc[t][:, hh, :], in0=OSv[:, 2 * t + hh, 0:64], scalar=rz2[:, 2 + hh:3 + hh], in1=acc[t][:, hh, :],
                                                                       op0=ALU.mult, op1=ALU.add), reads=[t_OS, t_rz2, t_acc[t]], writes=[t_acc[t]])
                for hh in range(2):
                    P.op("dve", lambda E, hh=hh: E.scalar_tensor_tensor(out=ob[t][:, hh * 64:(hh + 1) * 64], in0=OWv[:, 2 * t + hh, 0:64], scalar=rz2[:, 6 + hh:7 + hh], in1=acc[t][:, hh, :],
                                                                       op0=ALU.mult, op1=ALU.add), reads=[t_OW, t_rz2, t_acc[t]] + ([t_ob[t]] if hh else []), writes=[t_ob[t]])
                o = Tok(f"o{gt}")
                P.dma("pool", o_d[gt * 128:(gt + 1) * 128, :], ob[t][:], reads=[t_ob[t]], writes=[o], semname=f"o{t}")
                outs.append(o)

            def after(n=n):
                for t in range(2):
                    epi(t, 2 * n + t)
            push(qk, pv, after if kt == nks - 1 else None)
    flush()
    P.finish(outs)
    nc.all_engine_barrier()


def build_B():
    nc = _new_nc()
    emit_B(nc, IO(nc))
    return nc


def _bucket(dist):
    n = np.maximum(dist, 0)
    nf = np.maximum(n, 16).astype(np.float32)
    large = 16 + (np.log(nf / np.float32(16)) / np.float32(np.log(8.0)) * np.float32(16)).astype(np.int32)
    return np.where(n < 16, n, np.minimum(large, 31))


def _attn_consts():
    c = {}
    i = np.arange(1024)[:, None]
    nn = np.arange(256)[None, :]
    ovl = np.maximum(np.minimum(16 * i + 32, 64 * nn + 64) - np.maximum(16 * i, 64 * nn), 0).astype(np.float32) / 32.0
    ovl[1023] = 0
    c["ov"] = np.ascontiguousarray(ovl.reshape(8, 128, 256).transpose(1, 0, 2)).astype(NPBF)
    p = np.arange(128)[:, None, None]
    v = np.arange(64)[None, :, None]
    key = np.arange(128)[None, None, :]
    c["expall"] = (p == 2 * v + key // 64).astype(np.float32).astype(NPBF)
    q = np.arange(128)[:, None]
    rel = np.arange(512)[None, :] - 256
    jl = q // 64
    c["tv"] = (rel <= jl - 2).astype(np.float32)
    tfm = np.zeros((128, 512), np.float32)
    tfm[rel > jl + 0 * rel] = -1.0e9
    tfm[(rel == jl) & (rel == rel)] = 2.0e9
    tfm[rel == jl - 1] = 1.0e9
    c["tf"] = tfm
    c["tvalid"] = (rel <= jl + 0 * rel).astype(np.float32)
    r = np.arange(128)[:, None, None]
    qq = np.arange(128)[None, None, :]
    dist = np.zeros((NADD, 128, 4, 128), np.int64)
    vis = np.zeros((NADD, 128, 4, 128), bool)
    for d in range(17):
        dd = 128 * d + qq - 16 * r - 31 + np.zeros((1, 4, 1), np.int64)
        dist[d] = dd
        vis[d] = dd >= 0
    tt = np.array([0, 0, 1, 1])[None, :, None]
    for j in range(3):
        dd = 128 * (tt + 1 - j) + qq - r
        dist[17 + j] = dd
        vis[17 + j] = dd >= 0
    for ix, jp in enumerate([0, 1, 3, 4, 5]):
        dd = 128 * (tt + 4 - jp) + qq - r
        dist[20 + ix] = dd
        vis[20 + ix] = (dd >= 0) & (dd < 512)
    c["bucket"] = _bucket(dist)
    c["vis"] = vis
    return c


def run_Bc(kv, l, W):
    nc = _prog("Bc", build_Bc)
    kvu = kv.view(np.uint16)
    maps = []
    for r in range(NCORES):
        b, g, kvi = r // 4, (r // 2) % 2, r % 2
        xT = np.zeros((64, S + 16), np.uint16)
        xT[:, :S] = kvu[b * S:(b + 1) * S, kvi * 128 + g * 64: kvi * 128 + (g + 1) * 64].T
        sfx = "k" if kvi == 0 else "v"
        w1 = _tr(W["cmp_w1_" + sfx][l].reshape(32, 64, 256), (1, 0, 2))
        posT = _tr(W["cmp_pos_" + sfx][l], (1, 0))
        w2 = _tr(W["cmp_w2_" + sfx][l].reshape(2, 128, 64), (1, 0, 2))
        maps.append({"xT": xT.view(NPBF), "w1": w1, "posT": posT, "w2": w2})
    res = _run(nc, maps)
    out = {}
    for r in range(NCORES):
        b, g, kvi = r // 4, (r // 2) % 2, r % 2
        out[(b, g, kvi)] = (np.asarray(res[r]["ccT"]), np.asarray(res[r]["cc"]))
    return out


def run_B(q, kv, gates, cmp, W, AC):
    nc = _prog("B", build_B)
    qu = q.view(np.uint16)
    kvu = kv.view(np.uint16)
    T = W["rel_bias"]
    Tu = T.view(np.uint16)
    one = np.array(1.0, np.float32).astype(NPBF).view(np.uint16)
    negu = np.array(NEG, np.float32).astype(NPBF).view(np.uint16)
    maps = []
    for r in range(NCORES):
        b, g, hp = r // 4, (r // 2) % 2, r % 2
        mine = [g * 4 + hp * 2, g * 4 + hp * 2 + 1]
        oth = [g * 4 + (1 - hp) * 2, g * 4 + (1 - hp) * 2 + 1]
        heads = mine + oth
        Qb = qu[b * S:(b + 1) * S].reshape(NPAIR, 2, 128, 8, 64)[:, :, :, heads, :]
        Qb = Qb.reshape(NPAIR, 2, 128, 2, 2, 64)
        qd = np.empty((65, NPAIR, 2, 2, 2, 128), np.uint16)
        qd[:64] = Qb.transpose(5, 0, 3, 1, 4, 2)
        fbh = Tu[31, heads].reshape(2, 2)
        qd[64] = fbh[None, :, None, :, None]
        KVb = kvu[b * S:(b + 1) * S]

        def kT(col):
            a = np.empty((65, S), np.uint16)
            a[:64] = KVb[:, col + g * 64: col + (g + 1) * 64].T
            a[64] = one
            return a.view(NPBF)

        def vaug(col):
            a = np.empty((128, 128, 65), np.uint16)
            a[:, :, :64] = KVb[:, col + g * 64: col + (g + 1) * 64].reshape(128, 128, 64).transpose(1, 0, 2)
            a[:, :, 64] = one
            return a.view(NPBF)
        ccT = cmp[(b, g, 0)][0].view(np.uint16)
        cc = cmp[(b, g, 1)][1].view(np.uint16)
        kcT = np.empty((65, 1024), np.uint16)
        kcT[:64] = ccT
        kcT[64] = one
        vc = np.empty((128, 8, 65), np.uint16)
        vc[:, :, :64] = cc.reshape(8, 128, 64).transpose(1, 0, 2)
        vc[:, :, 64] = one
        vc[127, 7, :] = 0
        gsel = gates[b * S:(b + 1) * S, g * 12 + hp * 6: g * 12 + hp * 6 + 6]
        gts = np.ascontiguousarray(gsel.reshape(128, 128, 6).transpose(1, 0, 2))
        bg = np.empty((NADD, 128, 4, 128), np.uint16)
        hx_c = np.array(heads)
        hx_p = np.array(mine + mine)
        for i in range(NADD):
            hx = hx_c if i < 17 else hx_p
            g_ = Tu[AC["bucket"][i], hx[None, :, None]]
            bg[i] = np.where(AC["vis"][i], g_, negu)
        farb = np.empty((128, 2, 4), np.uint16)
        farb[:, 0, :] = Tu[31, hx_c][None, :]
        farb[:, 1, :] = Tu[31, hx_p][None, :]
        maps.append({"qd": qd.reshape(65, NPAIR, 1024).view(NPBF), "kTs": kT(256), "vs": vaug(384), "kTw": kT(512), "vw": vaug(640),
                     "kcT": kcT.view(NPBF), "vc": vc.view(NPBF), "gts": gts, "ov": AC["ov"], "expall": AC["expall"], "ident": AC["ident"],
                     "tv": AC["tv"], "tf": AC["tf"], "tvalid": AC["tvalid"],
                     "bg": np.ascontiguousarray(bg.transpose(1, 0, 2, 3)).reshape(128, NADD, 512).view(NPBF), "farb": farb.view(NPBF)})
    res = _run(nc, maps)
    nsa = np.empty((NTOK, 512), np.uint16)
    for r in range(NCORES):
        b, g, hp = r // 4, (r // 2) % 2, r % 2
        c0 = (g * 4 + hp * 2) * 64
        nsa[b * S:(b + 1) * S, c0:c0 + 128] = np.asarray(res[r]["o"]).view(np.uint16)
    return nsa.view(NPBF)


CAST_NAMES = ["w_in", "w_out", "w_gate_up", "w_down", "cmp_w1_k", "cmp_w1_v", "cmp_w2_k", "cmp_w2_v", "cmp_pos_k", "cmp_pos_v", "gmlp_w_s", "rel_bias"]


def build_X(first, last):
    nc = _new_nc()
    h2 = None
    if not first:
        h1 = _scratch(nc, "h1s", [TPC, D], F32)
        h2 = _dram(nc, "h2", [TPC, D], F32, out=True)
        emit_C1(nc, IO(nc, "c1_", {"h_o": h1}))
        emit_C2(nc, IO(nc, "c2_", {"h": h1, "h_o": h2}))
    if not last:
        emit_A(nc, IO(nc, "a_", {} if first else {"h": h2}))
    return nc


def build_Y():
    nc = _new_nc()
    kc_ccT = _scratch(nc, "kc_ccT", [64, 1024], BF16)
    kc_cc = _scratch(nc, "kc_cc", [1024, 64], BF16)
    vc_ccT = _scratch(nc, "vc_ccT", [64, 1024], BF16)
    vc_cc = _scratch(nc, "vc_cc", [1024, 64], BF16)
    emit_Bc(nc, IO(nc, "k_", {"ccT": kc_ccT, "cc": kc_cc}))
    emit_Bc(nc, IO(nc, "v_", {"ccT": vc_ccT, "cc": vc_cc}))
    emit_B(nc, IO(nc, ""), cmp_src=(kc_ccT, vc_cc))
    return nc


def run_X(l, first, last, h, mixed, W, C):
    nc = _prog("X", build_X, first, last)
    common = {}
    if not first:
        lp = l - 1
        common.update({"c1_w": _tr(W["w_out"][lp].reshape(8, 128, D), (1, 0, 2)), "c1_gpost": _rep(W["norm_mix_post"][lp]),
                       "c2_wgu": _tr(W["w_gate_up"][lp].reshape(8, 128, 2 * DFF), (1, 0, 2)),
                       "c2_wdn": _tr(W["w_down"][lp].reshape(NF, 128, D), (1, 0, 2)),
                       "c2_gpre": _rep(W["norm_ffn_pre"][lp]), "c2_gpost": _rep(W["norm_ffn_post"][lp]), "c2_ident": C["ident"]})
    if not last:
        common.update({"a_gpre": _rep(W["norm_mix_pre"][l]), "a_w": _tr(W["w_in"][l].reshape(8, 128, INW), (1, 0, 2)),
                       "a_lng": _rep(W["gmlp_ln_g"][l]), "a_lnb": _rep(W["gmlp_ln_b"][l]), "a_wsT": _tr(W["gmlp_w_s"][l], (2, 0, 1)),
                       "a_tri": C["tri"], "a_bsT": np.ascontiguousarray(W["gmlp_b_s"][l].T), "a_ident": C["ident"]})
    maps = []
    for r in range(NCORES):
        m = dict(common)
        if first:
            m["a_h"] = h[r * TPC:(r + 1) * TPC]
        else:
            m["c1_h"] = h[r * TPC:(r + 1) * TPC]
            mm_ = mixed[r * TPC:(r + 1) * TPC].reshape(NT, 128, 8, 128)
            m["c1_mixT"] = _tr(mm_, (3, 0, 2, 1))
        maps.append(m)
    res = _run(nc, maps)
    cat = lambda k: np.concatenate([np.asarray(res[r][k]) for r in range(NCORES)], 0)
    h2 = None if first else cat("h2")
    if last:
        return h2, None
    return h2, (cat("a_q_o"), cat("a_kv_o"), cat("a_g_o"), cat("a_gm_o"))


def run_Y(l, q, kv, gates, W, AC):
    nc = _prog("Y", build_Y)
    bf = np.asarray(q).dtype == NPBF
    qu = _raw(q)
    kvu = _raw(kv)
    Tu = _raw(W["rel_bias"])
    rdt = qu.dtype
    one = _const(1.0, bf)
    negu = _const(NEG, bf)
    cw = {}
    for sfx in ("k", "v"):
        cw[sfx] = {"w1": _tr(W["cmp_w1_" + sfx][l].reshape(32, 64, 256), (1, 0, 2)), "posT": _tr(W["cmp_pos_" + sfx][l], (1, 0)),
                   "w2": _tr(W["cmp_w2_" + sfx][l].reshape(2, 128, 64), (1, 0, 2))}
    onesrow = _unraw(np.full((1, 1024), one, rdt), bf)
    vc1 = np.full((128, 8, 1), one, rdt)
    vc1[127, 7, 0] = 0
    vc1 = _unraw(vc1, bf)
    zrow = _unraw(np.zeros((1, 65), rdt), bf)
    maps = []
    for r in range(NCORES):
        b, g, hp = r // 4, (r // 2) % 2, r % 2
        mine = [g * 4 + hp * 2, g * 4 + hp * 2 + 1]
        oth = [g * 4 + (1 - hp) * 2, g * 4 + (1 - hp) * 2 + 1]
        heads = mine + oth
        Qb = qu[b * S:(b + 1) * S].reshape(NPAIR, 2, 128, 8, 64)[:, :, :, heads, :]
        Qb = Qb.reshape(NPAIR, 2, 128, 2, 2, 64)
        qd = np.empty((65, NPAIR, 2, 2, 2, 128), rdt)
        qd[:64] = Qb.transpose(5, 0, 3, 1, 4, 2)
        qd[64] = Tu[31, heads].reshape(2, 2)[None, :, None, :, None]
        KVb = kvu[b * S:(b + 1) * S]

        def kT(col):
            a = np.empty((65, S), rdt)
            a[:64] = KVb[:, col + g * 64: col + (g + 1) * 64].T
            a[64] = one
            return _unraw(a, bf)

        def vaug(col):
            a = np.empty((128, 128, 65), rdt)
            a[:, :, :64] = KVb[:, col + g * 64: col + (g + 1) * 64].reshape(128, 128, 64).transpose(1, 0, 2)
            a[:, :, 64] = one
            return _unraw(a, bf)

        def xT(col):
            a = np.zeros((64, S + 16), rdt)
            a[:, :S] = KVb[:, col + g * 64: col + (g + 1) * 64].T
            return _unraw(a, bf)
        gsel = np.asarray(gates)[b * S:(b + 1) * S, g * 12 + hp * 6: g * 12 + hp * 6 + 6]
        gts = np.ascontiguousarray(gsel.reshape(128, 128, 6).transpose(1, 0, 2))
        bg = np.empty((NADD, 128, 4, 128), Tu.dtype)
        hx_c = np.array(heads)
        hx_p = np.array(mine + mine)
        for i in range(NADD):
            hx = hx_c if i < 17 else hx_p
            bg[i] = np.where(AC["vis"][i], Tu[AC["bucket"][i], hx[None, :, None]], negu)
        farb = np.empty((128, 2, 4), Tu.dtype)
        farb[:, 0, :] = Tu[31, hx_c][None, :]
        farb[:, 1, :] = Tu[31, hx_p][None, :]
        tbf = np.asarray(W["rel_bias"]).dtype == NPBF
        m = {"qd": _unraw(qd.reshape(65, NPAIR, 1024), bf), "kTs": kT(256), "vs": vaug(384), "kTw": kT(512), "vw": vaug(640),
             "gts": gts, "ov": AC["ov"], "expall": AC["expall"], "ident": AC["ident"],
             "tv": AC["tv"], "tf": AC["tf"], "tvalid": AC["tvalid"],
             "bg": _unraw(np.ascontiguousarray(bg.transpose(1, 0, 2, 3)).reshape(128, NADD, 512), tbf), "farb": _unraw(farb, tbf),
             "onesrow": onesrow, "vc1": vc1, "zrow": zrow,
             "k_xT": xT(0), "v_xT": xT(128)}
        for sfx in ("k", "v"):
            for nm in ("w1", "posT", "w2"):
                m[sfx + "_" + nm] = cw[sfx][nm]
        maps.append(m)
    res = _run(nc, maps)
    o0 = _raw(res[0]["o"])
    obf = np.asarray(res[0]["o"]).dtype == NPBF
    nsa = np.empty((NTOK, 512), o0.dtype)
    for r in range(NCORES):
        b, g, hp = r // 4, (r // 2) % 2, r % 2
        c0 = (g * 4 + hp * 2) * 64
        nsa[b * S:(b + 1) * S, c0:c0 + 128] = _raw(res[r]["o"])
    return _unraw(nsa, obf)


def kernel(**inputs):
    inp = {k: np.asarray(v) for k, v in inputs.items()}
    W = dict(inp)
    for n, a in zip(CAST_NAMES, cast_all([inp[n] for n in CAST_NAMES])):
        W[n] = a
    C = _consts()
    AC = _attn_consts()
    AC["ident"] = C["ident"]
    h = np.ascontiguousarray(inp["x"].reshape(NTOK, D).astype(np.float32))
    mixed = None
    for l in range(L_ + 1):
        h2, aout = run_X(l, l == 0, l == L_, h, mixed, W, C)
        if h2 is not None:
            h = h2
        if aout is None:
            break
        q, kv, gates, gm = aout
        nsa = run_Y(l, q, kv, gates, W, AC)
        if np.asarray(nsa).dtype == NPBF and np.asarray(gm).dtype == NPBF:
            mixed = np.concatenate([_raw(nsa), _raw(gm)], axis=1).view(NPBF)
        else:
            mixed = np.concatenate([np.asarray(nsa, np.float64), np.asarray(gm, np.float64)], axis=1)
    return np.asarray(h).reshape(B, S, D).astype(np.float32)
```
